# Optimizing a Trainium2 kernel written in Bass

```python
import math
import jax, jax.numpy as jnp
from jax import lax
import numpy as np

D_MODEL = 1024
BATCH = 16
SEQ = 2048
DEPTH = 1

HEAD_DIM = 64
HEADS_PER_GROUP = 4
DILATION_GROUPS = ((128, 1), (512, 4), (2048, 16))
N_ATT_GROUPS = len(DILATION_GROUPS)
N_ATT_HEADS = N_ATT_GROUPS * HEADS_PER_GROUP
ATT_WIDTH = N_ATT_HEADS * HEAD_DIM
ATT_OUT_WIDTH = HEADS_PER_GROUP * HEAD_DIM
BLOCK = 128
POOL_WINDOWS = (2, 4, 8, 16)
N_POOL_GROUPS = len(POOL_WINDOWS)
POOL_WIDTH = D_MODEL
POOL_GROUP_WIDTH = POOL_WIDTH // N_POOL_GROUPS
IN_WIDTH = 3 * ATT_WIDTH + POOL_WIDTH + 2 * D_MODEL
D_FF = -(-8 * D_MODEL // (3 * 256)) * 256
N_ADA = 6
DEEPNORM_ALPHA = (2.0 * DEPTH) ** 0.25
DEEPNORM_BETA = (8.0 * DEPTH) ** -0.25
LN_EPS = 1e-5

kernel_name = "hybrid_dilated_attn_pool_deepnorm_adaln"


def alibi_slopes(n):
    def pow2_slopes(m):
        start = 2.0 ** (-8.0 / m)
        return [start ** (i + 1) for i in range(m)]
    if math.log2(n).is_integer():
        s = pow2_slopes(n)
    else:
        c = 2 ** math.floor(math.log2(n))
        s = pow2_slopes(c) + pow2_slopes(2 * c)[0::2][: n - c]
    return jnp.asarray(np.array(sorted(s, reverse=True), dtype=np.float32))


def layer_norm(x, g, b):
    xf = x.astype(jnp.float32)
    mu = jnp.mean(xf, axis=-1, keepdims=True)
    var = jnp.mean(jnp.square(xf - mu), axis=-1, keepdims=True)
    y = (xf - mu) * lax.rsqrt(var + LN_EPS) * g.astype(jnp.float32) + b.astype(jnp.float32)
    return y.astype(x.dtype)


def dilated_window_attention(q, k, v, window, dilation, slopes):
    B, S, H, Dh = q.shape
    steps = window // dilation
    chunk = dilation * BLOCK
    s_pad = -(-S // chunk) * chunk
    L = s_pad // dilation
    nb = L // BLOCK

    def to_blocks(t):
        t = jnp.pad(t, ((0, 0), (0, s_pad - S), (0, 0), (0, 0)))
        t = t.reshape(B, L, dilation, H, Dh).transpose(0, 2, 3, 1, 4)
        return t.reshape(B, dilation, H, nb, BLOCK, Dh)

    def with_prev(t):
        prev = jnp.pad(t[:, :, :, :-1], ((0, 0), (0, 0), (0, 0), (1, 0), (0, 0), (0, 0)))
        return jnp.concatenate([prev, t], axis=4)

    qb = to_blocks(q)
    kw = with_prev(to_blocks(k))
    vw = with_prev(to_blocks(v))

    scores = jnp.einsum('brhnqd,brhnkd->brhnqk', qb, kw).astype(jnp.float32) * (1.0 / math.sqrt(Dh))
    qi = jnp.arange(BLOCK) + BLOCK
    kj = jnp.arange(2 * BLOCK)
    diff = qi[:, None] - kj[None, :]
    key_sub = jnp.arange(nb)[:, None, None] * BLOCK - BLOCK + kj[None, None, :]
    valid = (diff >= 0)[None] & (diff <= steps)[None] & (key_sub >= 0)
    bias = -slopes.astype(jnp.float32)[:, None, None] * (diff * dilation).astype(jnp.float32)[None]
    scores = scores + bias[None, None, :, None]
    scores = jnp.where(valid[None, None, None], scores, -jnp.inf)
    m = jnp.max(scores, axis=-1, keepdims=True)
    p = jnp.exp(scores - m)
    denom = jnp.sum(p, axis=-1, keepdims=True)
    out = jnp.einsum('brhnqk,brhnkd->brhnqd', p, vw.astype(jnp.float32)) / denom
    lse = (m + jnp.log(denom))[..., 0]

    out = out.reshape(B, dilation, H, L, Dh).transpose(0, 3, 1, 2, 4).reshape(B, s_pad, H, Dh)[:, :S]
    lse = lse.reshape(B, dilation, H, L).transpose(0, 3, 1, 2).reshape(B, s_pad, H)[:, :S]
    return out, lse


def causal_multiscale_pool(u):
    B, S, G, C = u.shape
    uf = u.astype(jnp.float32)
    cs = jnp.cumsum(uf, axis=1)
    pos = jnp.arange(S)
    outs = []
    for g, w in enumerate(POOL_WINDOWS):
        csg = cs[:, :, g]
        lagged = jnp.pad(csg, ((0, 0), (w, 0), (0, 0)))[:, :S]
        count = jnp.minimum(pos + 1, w).astype(jnp.float32)[None, :, None]
        outs.append((csg - lagged) / count)
    return jnp.stack(outs, axis=2) - uf


def setup_inputs(seed: int = 0) -> dict:
    key = jax.random.key(seed)
    ks = jax.random.split(key, 20)
    f32 = jnp.float32
    nrm = lambda k, shape, s: jax.random.normal(k, shape, f32) * s
    w_in = nrm(ks[3], (DEPTH, D_MODEL, IN_WIDTH), D_MODEL ** -0.5)
    w_in = w_in.at[:, :, 2 * ATT_WIDTH:3 * ATT_WIDTH].multiply(DEEPNORM_BETA)
    return {
        "x": nrm(ks[0], (BATCH, SEQ, D_MODEL), 1.0),
        "c": nrm(ks[1], (BATCH, D_MODEL), 1.0),
        "w_ada": nrm(ks[2], (DEPTH, D_MODEL, N_ADA * D_MODEL), 0.5 * D_MODEL ** -0.5),
        "b_ada": nrm(ks[4], (DEPTH, N_ADA * D_MODEL), 0.02),
        "w_in": w_in,
        "w_branch_att": nrm(ks[5], (DEPTH, ATT_OUT_WIDTH, D_MODEL), ATT_OUT_WIDTH ** -0.5),
        "w_pool_group": nrm(ks[6], (DEPTH, N_POOL_GROUPS, POOL_GROUP_WIDTH, POOL_GROUP_WIDTH), POOL_GROUP_WIDTH ** -0.5),
        "pool_scale": 1.0 + nrm(ks[7], (DEPTH, POOL_WIDTH), 0.1),
        "w_branch_pool": nrm(ks[8], (DEPTH, POOL_WIDTH, D_MODEL), POOL_WIDTH ** -0.5),
        "w_out": nrm(ks[9], (DEPTH, D_MODEL, D_MODEL), DEEPNORM_BETA * D_MODEL ** -0.5),
        "ln1_g": 1.0 + nrm(ks[10], (DEPTH, D_MODEL), 0.02),
        "ln1_b": nrm(ks[11], (DEPTH, D_MODEL), 0.02),
        "w_gate": nrm(ks[12], (DEPTH, D_MODEL, D_FF), D_MODEL ** -0.5),
        "w_up": nrm(ks[13], (DEPTH, D_MODEL, D_FF), D_MODEL ** -0.5),
        "w_down": nrm(ks[14], (DEPTH, D_FF, D_MODEL), DEEPNORM_BETA * D_FF ** -0.5),
        "ln2_g": 1.0 + nrm(ks[15], (DEPTH, D_MODEL), 0.02),
        "ln2_b": nrm(ks[16], (DEPTH, D_MODEL), 0.02),
    }


def reference(x, c, w_ada, b_ada, w_in, w_branch_att, w_pool_group, pool_scale, w_branch_pool,
              w_out, ln1_g, ln1_b, w_gate, w_up, w_down, ln2_g, ln2_b):
    B, S, D = x.shape
    slopes = alibi_slopes(N_ATT_HEADS).reshape(N_ATT_GROUPS, HEADS_PER_GROUP)
    split_at = [ATT_WIDTH, 2 * ATT_WIDTH, 3 * ATT_WIDTH, 3 * ATT_WIDTH + POOL_WIDTH,
                3 * ATT_WIDTH + POOL_WIDTH + D_MODEL]
    for l in range(DEPTH):
        mod = (jax.nn.silu(c) @ w_ada[l] + b_ada[l])[:, None, :]
        sh1, sc1, g1, sh2, sc2, g2 = jnp.split(mod, N_ADA, axis=-1)

        h = x * (1.0 + sc1) + sh1
        proj = h @ w_in[l]
        q, k, v, pool_in, ga, gb = jnp.split(proj, split_at, axis=-1)
        q = q.reshape(B, S, N_ATT_GROUPS, HEADS_PER_GROUP, HEAD_DIM)
        k = k.reshape(B, S, N_ATT_GROUPS, HEADS_PER_GROUP, HEAD_DIM)
        v = v.reshape(B, S, N_ATT_GROUPS, HEADS_PER_GROUP, HEAD_DIM)

        outs, lses = [], []
        for g, (window, dilation) in enumerate(DILATION_GROUPS):
            o, s = dilated_window_attention(q[:, :, g], k[:, :, g], v[:, :, g], window, dilation, slopes[g])
            outs.append(o)
            lses.append(s)
        mix_w = jax.nn.softmax(jnp.stack(lses, axis=0), axis=0)
        att = jnp.sum(mix_w[..., None] * jnp.stack(outs, axis=0), axis=0)
        att = att.reshape(B, S, ATT_OUT_WIDTH).astype(x.dtype)
        branch_a = att @ w_branch_att[l]

        u = pool_in.reshape(B, S, N_POOL_GROUPS, POOL_GROUP_WIDTH)
        pm = causal_multiscale_pool(u).astype(x.dtype)
        pg = jnp.einsum('bsgc,gce->bsge', pm, w_pool_group[l]).reshape(B, S, POOL_WIDTH) * pool_scale[l]
        branch_b = pg @ w_branch_pool[l]

        merged = jax.nn.sigmoid(ga) * branch_a + jax.nn.sigmoid(gb) * branch_b
        mixer_out = merged @ w_out[l]
        x = layer_norm(DEEPNORM_ALPHA * x + g1 * mixer_out, ln1_g[l], ln1_b[l])

        h2 = x * (1.0 + sc2) + sh2
        ffn = (jax.nn.silu(h2 @ w_gate[l]) * (h2 @ w_up[l])) @ w_down[l]
        x = layer_norm(DEEPNORM_ALPHA * x + g2 * ffn, ln2_g[l], ln2_b[l])
    return x
```

```python
import contextlib
import math

import numpy as np

import concourse.bass as bass
import concourse.mybir as mybir
from concourse.ap import AP
from concourse.bass_utils import run_bass_kernel_spmd

F32 = mybir.dt.float32
BF16 = mybir.dt.bfloat16
ALU = mybir.AluOpType
AF = mybir.ActivationFunctionType

P = 128
D = 1024
KC = 8
S = 2048
HALF = 1024
NB = 2
N_CORES = 8
IN_W = 5376
DFF = 2816
NFC = 22
ALPHA = 2.0 ** 0.25
LN_EPS = 1e-5
DIL = (1, 4, 16)
NBLK = (16, 4, 1)
POOL_W = (2, 4, 8, 16)
NW = 6
FGROUPS = ((0, 8), (8, 15), (15, 22))


def alibi_slopes_np(n):
    def pow2(m):
        start = 2.0 ** (-8.0 / m)
        return [start ** (i + 1) for i in range(m)]
    if math.log2(n).is_integer():
        s = pow2(n)
    else:
        c = 2 ** math.floor(math.log2(n))
        s = pow2(c) + pow2(2 * c)[0::2][: n - c]
    return np.array(sorted(s, reverse=True), dtype=np.float32)


def const_tables():
    slopes = alibi_slopes_np(12).reshape(3, 4)
    k = np.arange(128)[:, None]
    c = np.arange(256)[None, :]
    diff = np.where(c < 128, c - k, 128 + (c - 128) - k)
    valid = (diff >= 0) & (diff <= 128)
    bias = np.empty((128, 12, 256), np.float32)
    for g in range(3):
        for h in range(4):
            b = -(slopes[g, h] * (diff * DIL[g]).astype(np.float32)).astype(np.float32)
            bias[:, g * 4 + h, :] = np.where(valid, b, np.float32(-30000.0))
    tp = np.arange(128)[:, None]
    t = np.arange(128)[None, :]
    bands = np.zeros((128, 12, 128), np.float32)
    corr = np.zeros((128, 4, 16), np.float32)
    for g, w in enumerate(POOL_W):
        inwin = ((t - tp) >= 0) & ((t - tp) < w)
        eye = (t == tp)
        bands[:, g * 3 + 0, :] = inwin.astype(np.float32) - w * eye
        prev = ((t + 128 - tp) >= 0) & ((t + 128 - tp) < w)
        bands[:, g * 3 + 1, :] = prev.astype(np.float32)
        cnt = np.minimum(t + 1, w).astype(np.float32)
        bands[:, g * 3 + 2, :] = inwin.astype(np.float32) - cnt * eye
        corr[:, g, :] = (1.0 / np.minimum(np.arange(16) + 1, w)).astype(np.float32)[None, :]
    return bias, bands, corr


class Tl:
    def __init__(self, sem):
        self.sem = sem
        self.cnt = 0


class Reg:
    __slots__ = ("w", "r")

    def __init__(self):
        self.w = None
        self.r = {}


class Eng:
    def __init__(self, eng, sem, is_pe=False):
        self.eng = eng
        self.tl = Tl(sem)
        self.seen = {}
        self.is_pe = is_pe


class Arena:
    def __init__(self, t, n_elems, gran):
        self.ap = t[:]
        self.tensor = self.ap.tensor
        self.pitch = self.ap.ap[0][0]
        self.gran = gran
        self.regs = [Reg() for _ in range((n_elems + gran - 1) // gran)]

    def R(self, off, n):
        return self.regs[off // self.gran:(off + n - 1) // self.gran + 1]

    def A(self, off, dims, p0=0, np_=P):
        return AP(self.tensor, p0 * self.pitch + off, [[self.pitch, np_]] + [list(d) for d in dims])


class K:
    def __init__(self, nc, es):
        self.nc = nc
        self.es = es
        self.nsem = 0
        self.pe = Eng(nc.tensor, self.sem("tl_pe"), True)
        self.act = Eng(nc.scalar, self.sem("tl_act"))
        self.dve = Eng(nc.vector, self.sem("tl_dve"))
        self.pool = Eng(nc.gpsimd, self.sem("tl_pool"))
        self.sp = Eng(nc.sync, self.sem("tl_sp"))
        self.flip = 0
        self.dry = False

    def sem(self, name):
        self.nsem += 1
        return self.es.enter_context(self.nc.semaphore(name))

    def sb(self, name, shape, dt):
        return self.es.enter_context(self.nc.sbuf_tensor(name, shape, dt))

    def _waits(self, e, reads, writes):
        deps = {}

        def add(tok, raw):
            if tok is None:
                return
            tl, v = tok
            if tl is e.tl and e.is_pe:
                return
            if deps.get(tl, 0) < v:
                deps[tl] = v
        for r in reads:
            add(r.w, True)
        for w in writes:
            add(w.w, False)
            for tok in w.r.values():
                add(tok, False)
        for tl, v in deps.items():
            if e.seen.get(tl, 0) < v:
                e.eng.wait_ge(tl.sem, v)
                e.seen[tl] = v

    def op(self, e, fn, reads=(), writes=(), inc=True):
        if self.dry:
            return None
        self._waits(e, reads, writes)
        tick = e.tl.cnt + 1
        ins = fn()
        if inc:
            ins.then_inc(e.tl.sem, 1)
            e.tl.cnt = tick
        tok = (e.tl, tick)
        for r in reads:
            r.r[e.tl] = tok
        for w in writes:
            w.w = tok
            w.r = {}
        return ins

    def dma(self, q, tl, xfers, reads=(), writes=()):
        if self.dry:
            return
        self._waits(q, reads, writes)
        if tl.cnt and q.seen.get(tl, 0) < tl.cnt:
            q.eng.wait_ge(tl.sem, tl.cnt)
            q.seen[tl] = tl.cnt
        for (o, i) in xfers:
            q.eng.dma_start(out=o, in_=i).then_inc(tl.sem, 16)
            tl.cnt += 16
        tok = (tl, tl.cnt)
        for r in reads:
            r.r[tl] = tok
        for w in writes:
            w.w = tok
            w.r = {}

    def evac_eng(self):
        self.flip ^= 1
        return self.act if self.flip else self.dve


class _Stop(Exception):
    pass


def build_program(debug=False, stop_after=None):
    nc = bass.Bass("TRN2", target_bir_lowering=False)

    def chk(name):
        if stop_after == name:
            raise _Stop()

    def din(name, shape):
        return nc.dram_tensor(name, shape, F32, kind="ExternalInput").ap()
    xT = din("xT", [NB, D, S])
    cT = din("cT", [P, KC, NB])
    w_ada = din("w_ada", [D, 6 * D])
    b_adaT = din("b_adaT", [P, 48])
    w_in = din("w_in", [D, IN_W])
    w_batt = din("w_batt", [256, D])
    w_pg = din("w_pg", [4, 256, 256])
    pscaleT = din("pscaleT", [P, KC])
    w_bp = din("w_bp", [D, D])
    w_out = din("w_out", [D, D])
    ln1gT = din("ln1gT", [P, KC])
    ln1bT = din("ln1bT", [P, KC])
    ln2gT = din("ln2gT", [P, KC])
    ln2bT = din("ln2bT", [P, KC])
    w_gate = din("w_gate", [D, DFF])
    w_up = din("w_up", [D, DFF])
    w_down = din("w_down", [DFF, D])
    bias_d = din("bias_tab", [P, 12, 256])
    bands_d = din("bands", [P, 12, 128])
    corr_d = din("corr", [P, 4, 16])
    outT = nc.dram_tensor("outT", [NB, D, S], F32, kind="ExternalOutput").ap()
    dbg = {}
    if debug:
        dbg["modT"] = nc.dram_tensor("dbg_modT", [P, 48, NB], F32, kind="ExternalOutput").ap()
        dbg["hT"] = nc.dram_tensor("dbg_hT", [P, KC, S], BF16, kind="ExternalOutput").ap()
        dbg["attT"] = nc.dram_tensor("dbg_attT", [P, 2, S], BF16, kind="ExternalOutput").ap()
        dbg["pmT"] = nc.dram_tensor("dbg_pmT", [P, KC, HALF], BF16, kind="ExternalOutput").ap()
        dbg["pgT"] = nc.dram_tensor("dbg_pgT", [P, KC, HALF], BF16, kind="ExternalOutput").ap()
        dbg["mrg"] = nc.dram_tensor("dbg_mrg", [P, KC, HALF], BF16, kind="ExternalOutput").ap()
        dbg["y1"] = nc.dram_tensor("dbg_y1", [P, KC, HALF], F32, kind="ExternalOutput").ap()
        dbg["h2"] = nc.dram_tensor("dbg_h2", [P, KC, HALF], BF16, kind="ExternalOutput").ap()

    w_in_v = w_in.rearrange("(k p) n -> p k n", p=P)
    w_ada_v = w_ada.rearrange("(k p) n -> p k n", p=P)
    w_bp_v = w_bp.rearrange("(k p) n -> p k n", p=P)
    w_out_v = w_out.rearrange("(k p) n -> p k n", p=P)
    w_gate_v = w_gate.rearrange("(k p) n -> p k n", p=P)
    w_up_v = w_up.rearrange("(k p) n -> p k n", p=P)
    w_down_v = w_down.rearrange("(k p) n -> p k n", p=P)
    w_batt_v = w_batt.rearrange("(k p) n -> p k n", p=P)
    w_pg_v = w_pg.rearrange("g (c p) o -> p c g o", p=P)

    with contextlib.ExitStack() as es:
        k = K(nc, es)
        PE, ACT, DVE, POOL, SP = k.pe, k.act, k.dve, k.pool, k.sp

        bias_t = k.sb("bias_t", [P, 12, 256], F32)
        bands_t = k.sb("bands_t", [P, 12, 128], BF16)
        corr_t = k.sb("corr_t", [P, 4, 16], F32)
        onesm = k.sb("onesm", [P, P], BF16)
        cT_t = k.sb("cT_t", [P, KC, NB], F32)
        scT_t = k.sb("scT_t", [P, KC, NB], BF16)
        badaT_t = k.sb("badaT_t", [P, 48], F32)
        modT = k.sb("modT", [P, 48, NB], F32)
        prm = k.sb("prm", [P, 5, KC], F32)
        A1 = k.sb("A1", [P, KC, NB], F32)
        A2 = k.sb("A2", [P, KC, NB], F32)
        G2 = k.sb("G2", [P, KC, NB], F32)
        B2 = k.sb("B2", [P, KC, NB], F32)
        aGB = k.sb("aGB", [P, 2, KC], F32)
        watt = k.sb("watt", [P, 2, D], BF16)
        hT = k.sb("hT", [P, KC, S], BF16)
        attT = k.sb("attT", [P, 2, S], BF16)
        wring = k.sb("wring", [P, NW, KC * 256], BF16)
        T1 = k.sb("T1", [P, 8192], F32)
        T2 = k.sb("T2", [P, 8192], BF16)
        T3 = k.sb("T3", [P, 8192], BF16)
        T4 = k.sb("T4", [P, 4096], F32)
        u_t = k.sb("u_t", [P, 2, 9, 256], BF16)
        sg_t = k.sb("sg_t", [P, 4, 512], F32)
        t12_t = k.sb("t12_t", [P, 2, 512], F32)
        ybs_t = k.sb("ybs_t", [P, 6, 512], BF16)
        st_t = k.sb("st_t", [P, 4, 512], F32)
        ost_t = k.sb("ost_t", [P, 2, 512], F32)
        eps_t = k.sb("eps_t", [P, 1], F32)

        psum = [es.enter_context(nc.psum_tensor(f"ps{i}", [P, 512], F32)) for i in range(8)]
        ps_reg = [Reg() for _ in range(8)]
        ps_held = [False] * 8
        ps_next = [0]

        def new_bank(hold=False):
            for _ in range(16):
                i = ps_next[0]
                ps_next[0] = (i + 1) % 8
                if not ps_held[i]:
                    ps_held[i] = hold
                    return i
            raise RuntimeError("no psum bank")

        aT1 = Arena(T1, 8192, 512)
        aT2 = Arena(T2, 8192, 512)
        aT3 = Arena(T3, 8192, 512)
        aT4 = Arena(T4, 4096, 256)
        T4b = T4[:].bitcast(BF16)
        T4b_pitch = T4b.ap[0][0]

        def actT_ap(fl, off, n):
            return AP(T4b.tensor, T4b.offset + fl * 1024 + off, [[T4b_pitch, P], [1, n]])

        r_hT = [[Reg() for _ in range(4)] for _ in range(KC)]
        r_attT = [[Reg() for _ in range(4)] for _ in range(2)]
        r_w = [Reg() for _ in range(NW)]
        r_const = Reg()
        r_m1 = Reg()
        r_m2 = Reg()
        r_u = [Reg(), Reg()]
        r_sg = [Reg() for _ in range(4)]
        r_t12 = [Reg(), Reg()]
        r_ybs = [Reg() for _ in range(6)]
        r_st = [Reg() for _ in range(4)]
        r_ost = [Reg() for _ in range(2)]
        r_watt = Reg()
        r_bands = Reg()

        tl_w = [Tl(k.sem(f"dw{i}")) for i in range(NW)]
        tl_c = Tl(k.sem("dconst"))
        tl_c2 = Tl(k.sem("dconst2"))
        tl_x = [Tl(k.sem(f"dx{i}")) for i in range(4)]
        tl_o = [Tl(k.sem(f"do{i}")) for i in range(2)]
        tl_dbgs = {}

        def tl_dbg_for(name):
            if k.dry:
                return None
            tl_dbgs[name] = Tl(k.sem("ddbg_" + name))
            return tl_dbgs[name]
        tl_y = [Tl(k.sem(f"dy{i}")) for i in range(8)]

        w_next = [0]

        def wslot():
            i = w_next[0]
            w_next[0] = (i + 1) % NW
            return i

        def wv(slot, kk, c0, n):
            return wring[:, slot, kk * 256 + c0: kk * 256 + c0 + n]

        plan = []
        wst = {"req": 0, "emitted": 0}

        def item_begin():
            if k.dry:
                return
            hi = min(len(plan), wst["req"] + NW)
            while wst["emitted"] < hi:
                j = wst["emitted"]
                s = j % NW
                k.dma(POOL, tl_w[s], plan[j](s), writes=[r_w[s]])
                wst["emitted"] += 1

        def wload(xfers_fn):
            if k.dry:
                plan.append(xfers_fn)
                return (len(plan) - 1) % NW
            j = wst["req"]
            assert j < wst["emitted"], "wload without item_begin"
            wst["req"] += 1
            return j % NW

        def wdst(slot, nk, c0, n):
            return wring[:, slot, :].rearrange("p (k c) -> p k c", c=256)[:, 0:nk, c0:c0 + n]

        def emit_consts():
            k.dma(SP, tl_c, [
                (bias_t[:], bias_d), (corr_t[:], corr_d), (cT_t[:], cT), (badaT_t[:], b_adaT),
                (prm[:, 0, :], ln1gT), (prm[:, 1, :], ln1bT), (prm[:, 2, :], ln2gT), (prm[:, 3, :], ln2bT),
                (prm[:, 4, :], pscaleT),
            ], writes=[r_const])
            k.dma(POOL, tl_c2, [(bands_t[:], bands_d), (watt[:], w_batt_v)], writes=[r_bands, r_watt])
            k.op(DVE, lambda: nc.vector.memset(onesm[:], 1.0 / D), writes=[r_const])
            k.op(DVE, lambda: nc.vector.memset(eps_t[:], LN_EPS), writes=[r_const])
            k.op(ACT, lambda: nc.scalar.activation(scT_t[:], cT_t[:], AF.Silu), reads=[r_const], writes=[r_m1])
            k.op(DVE, lambda: nc.vector.tensor_scalar(aGB[:], prm[:, 0:2, :], ALPHA, None, ALU.mult), reads=[r_const], writes=[r_m1])

        def emit_mod(it):
            rm = r_m1 if it < 8 else r_m2
            item_begin()
            sl = wload(lambda s: [(wdst(s, KC, 0, 256), w_ada_v[:, :, 256 * it:256 * it + 256])])
            bk = new_bank()
            for mc in range(2):
                for kk in range(KC):
                    k.op(PE, lambda kk=kk, mc=mc: nc.tensor.matmul(
                        psum[bk][:, mc * 2:mc * 2 + 2], wv(sl, kk, mc * 128, 128), scT_t[:, kk, :], start=(kk == 0), stop=(kk == KC - 1)),
                        reads=[r_w[sl], r_m1], writes=[ps_reg[bk]], inc=(mc == 1 and kk == KC - 1))
            j0 = 2 * it
            k.op(DVE, lambda: nc.vector.tensor_tensor(
                modT[:, j0:j0 + 2, :], psum[bk][:, 0:4].rearrange("p (j b) -> p j b", b=NB),
                badaT_t[:, j0:j0 + 2].unsqueeze(2).broadcast_to([P, 2, NB]), ALU.add),
                reads=[ps_reg[bk], r_const], writes=[rm])

        def emit_derived1():
            k.op(DVE, lambda: nc.vector.tensor_scalar(A1[:], modT[:, 8:16, :], 1.0, None, ALU.add), reads=[r_m1], writes=[r_m1])

        def emit_derived2():
            k.op(DVE, lambda: nc.vector.tensor_scalar(A2[:], modT[:, 32:40, :], 1.0, None, ALU.add), reads=[r_m2], writes=[r_m2])
            k.op(DVE, lambda: nc.vector.tensor_tensor(
                G2[:], A2[:], prm[:, 0, :].unsqueeze(2).broadcast_to([P, KC, NB]), ALU.mult), reads=[r_m2, r_const], writes=[r_m2])
            k.op(DVE, lambda: nc.vector.tensor_tensor(
                B2[:], A2[:], prm[:, 1, :].unsqueeze(2).broadcast_to([P, KC, NB]), ALU.mult), reads=[r_m2, r_const], writes=[r_m2])
            k.op(DVE, lambda: nc.vector.tensor_tensor(B2[:], B2[:], modT[:, 24:32, :], ALU.add), reads=[r_m2], writes=[r_m2])
            if debug:
                k.dma(SP, tl_dbg_for("modT"), [(dbg["modT"], modT[:])], reads=[r_m1, r_m2])

        def sc(t, m, b):
            return t[:, m, b:b + 1]

        def mm_group(bank, cols, pairs, reads, last_inc=True):
            n = len(pairs)
            for i, (lhsT, rhs) in enumerate(pairs):
                k.op(PE, lambda lhsT=lhsT, rhs=rhs, i=i: nc.tensor.matmul(
                    psum[bank][:, cols[0]:cols[1]], lhsT, rhs, start=(i == 0), stop=(i == n - 1)),
                    reads=reads, writes=[ps_reg[bank]], inc=(last_inc and i == n - 1))

        def tok_slice(g, gb):
            d, nb = DIL[g], NBLK[g]
            r, n = gb // nb, gb % nb
            start = r + d * 128 * n
            return slice(start, start + d * 127 + 1, d)

        T2f = T2[:].bitcast(F32)
        T2f_pitch = T2f.ap[0][0]
        s0_done = set()

        def emit_S0(b, early=False):
            if b in s0_done:
                return
            s0_done.add(b)
            for m in range(KC):
                if early:
                    q = m % 2
                    regs = aT2.R(q * 4096, 4096)
                    stg = AP(T2f.tensor, T2f.offset + q * 2048, [[T2f_pitch, P], [1, S]])
                else:
                    q = m % 4
                    regs = aT1.R(q * 2048, 2048)
                    stg = aT1.A(q * 2048, [[1, S]])
                k.dma(SP, tl_x[q], [(stg, xT[b, m * P:(m + 1) * P, :])], writes=regs)
                if m % 2 == 0:
                    k.op(ACT, lambda m=m, stg=stg: nc.scalar.activation(
                        hT[:, m, :], stg, AF.Identity, bias=sc(modT, m, b), scale=sc(A1, m, b)),
                        reads=regs + [r_m1], writes=r_hT[m])
                else:
                    k.op(DVE, lambda m=m, stg=stg: nc.vector.tensor_scalar(
                        hT[:, m, :], stg, sc(A1, m, b), sc(modT, m, b), ALU.mult, ALU.add),
                        reads=regs + [r_m1], writes=r_hT[m])

        def seq_body(b):
            emit_S0(b)
            if debug and b == 0:
                k.dma(SP, tl_dbg_for("hT"), [(dbg["hT"], hT[:])], reads=[r for rr in r_hT for r in rr])

            chk('S0')
            k.op(DVE, lambda: nc.vector.memset(aT3.A(64, [[384, 16], [192, 2], [1, 64]]), 1.0), writes=aT3.R(0, 6144))

            for g in range(3):
                d, nb = DIL[g], NBLK[g]
                item_begin()
                sq = wload(lambda s, g=g: [(wdst(s, KC, 0, 256), w_in_v[:, :, 256 * g:256 * g + 256])])
                sk = wload(lambda s, g=g: [(wdst(s, KC, 0, 256), w_in_v[:, :, 768 + 256 * g:768 + 256 * g + 256])])
                sv = wload(lambda s, g=g: [(wdst(s, KC, 0, 256), w_in_v[:, :, 1536 + 256 * g:1536 + 256 * g + 256])])
                for mc in range(4):
                    slot = sq if mc < 2 else sk
                    c0 = (mc % 2) * 128
                    for tt in range(4):
                        bk = new_bank()
                        mm_group(bk, (0, 512), [(wv(slot, kk, c0, 128), hT[:, kk, tt * 512:(tt + 1) * 512]) for kk in range(KC)],
                                 reads=[r_w[slot]] + [r_hT[kk][tt] for kk in range(KC)])
                        if g == 0:
                            src = psum[bk][:, :]
                            dst = aT2.A(mc * 2048 + 512 * tt, [[1, 512]])
                        elif g == 1:
                            src = AP(psum[bk][:].tensor, psum[bk][:].offset, [[512, P], [1, 4], [4, 128]])
                            dst = aT2.A(mc * 2048 + 128 * tt, [[512, 4], [1, 128]])
                        else:
                            src = AP(psum[bk][:].tensor, psum[bk][:].offset, [[512, P], [1, 16], [16, 32]])
                            dst = aT2.A(mc * 2048 + 32 * tt, [[128, 16], [1, 32]])
                        regs = aT2.R(mc * 2048, 2048)
                        scl = 0.125 if mc < 2 else 1.0
                        e = k.evac_eng()
                        if e is ACT:
                            k.op(ACT, lambda dst=dst, src=src, scl=scl: nc.scalar.activation(dst, src, AF.Copy, scale=scl),
                                 reads=[ps_reg[bk]], writes=regs)
                        else:
                            k.op(DVE, lambda dst=dst, src=src, scl=scl: nc.vector.tensor_scalar(dst, src, scl, None, ALU.mult),
                                 reads=[ps_reg[bk]], writes=regs)
                chk('qk%d' % g)
                for gb in range(0, 16, 2):
                    bk = new_bank()
                    for j in range(2):
                        ts = tok_slice(g, gb + j)
                        tts = sorted(set([ts.start // 512, (ts.stop - 1) // 512])) if g == 0 else range(4)
                        mm_group(bk, (j * 256, j * 256 + 256), [(hT[:, kk, ts], wv(sv, kk, 0, 256)) for kk in range(KC)],
                                 reads=[r_w[sv]] + [r_hT[kk][t_] for kk in range(KC) for t_ in tts], last_inc=(j == 1))
                    for j in range(2):
                        src = AP(psum[bk][:].tensor, psum[bk][:].offset + j * 256, [[512, P], [128, 2], [64, 2], [1, 64]])
                        dst = aT3.A((gb + j) * 384, [[192, 2], [128, 2], [1, 64]])
                        e = k.evac_eng()
                        regs = aT3.R((gb + j) * 384, 384)
                        if e is ACT:
                            k.op(ACT, lambda dst=dst, src=src: nc.scalar.copy(dst, src), reads=[ps_reg[bk]], writes=regs)
                        else:
                            k.op(DVE, lambda dst=dst, src=src: nc.vector.tensor_copy(dst, src), reads=[ps_reg[bk]], writes=regs)

                chk('v%d' % g)
                if b == 0 and g < 2:
                    for it_ in range(8 + 8 * g, 16 + 8 * g):
                        emit_mod(it_)
                    if g == 1:
                        emit_derived2()
                def pt_ap(sidx, rel, dims, np_=P):
                    if sidx < 4:
                        return aT3.A(6144 + sidx * 512 + rel, dims)
                    return AP(T4b.tensor, T4b.offset + 6144 + (sidx - 4) * 512 + rel, [[T4b_pitch, np_]] + [list(d_) for d_ in dims])

                def pt_regs(sidx):
                    if sidx < 4:
                        return aT3.R(6144 + sidx * 512, 512)
                    return aT4.R(3072 + (sidx - 4) * 256, 256)

                for hp in range(2):
                    qreg = aT2.R(hp * 2048, 2048)
                    kreg = aT2.R((2 + hp) * 2048, 2048)

                    def att_scores(pr, hp=hp, qreg=qreg, kreg=kreg):
                        gb2 = 2 * pr
                        ws = [256 if ((gb2 + j) % nb) < nb - 1 else 128 for j in range(2)]
                        sbk = [new_bank(), new_bank()]
                        for j in range(2):
                            gb = gb2 + j
                            for e_ in range(2):
                                kT = aT2.A((2 + hp) * 2048 + gb * 128, [[1, 128]], p0=64 * e_, np_=64)
                                qT = aT2.A(hp * 2048 + gb * 128, [[1, ws[j]]], p0=64 * e_, np_=64)
                                k.op(PE, lambda kT=kT, qT=qT, e_=e_, j=j: nc.tensor.matmul(
                                    psum[sbk[e_]][:, j * 256:j * 256 + ws[j]], kT, qT, start=True, stop=True),
                                    reads=qreg + kreg, writes=[ps_reg[sbk[e_]]], inc=(j == 1))
                        for e_ in range(2):
                            h = g * 4 + 2 * hp + e_
                            sidx = 3 * e_ + pr % 3
                            treg = aT4.R(2048 + e_ * 512, 512)
                            preg = pt_regs(sidx)
                            if ws[0] == ws[1]:
                                segs = [(0, 2, ws[0])]
                            else:
                                segs = [(0, 1, ws[0]), (1, 1, ws[1])]
                            for (j0, nj, w) in segs:
                                tmp = aT4.A(2048 + e_ * 512 + j0 * 256, [[256, nj], [1, w]])
                                src = AP(psum[sbk[e_]][:].tensor, psum[sbk[e_]][:].offset + j0 * 256, [[512, P], [256, nj], [1, w]])
                                bsrc = bias_t[:, h:h + 1, 0:w].broadcast_to([P, nj, w])
                                k.op(DVE, lambda tmp=tmp, src=src, bsrc=bsrc: nc.vector.tensor_tensor(tmp, src, bsrc, ALU.add),
                                     reads=[ps_reg[sbk[e_]], r_const], writes=treg)
                                pt = pt_ap(sidx, j0 * 256, [[256, nj], [1, w]])
                                k.op(ACT, lambda pt=pt, tmp=tmp: nc.scalar.activation(pt, tmp, AF.Exp), reads=treg, writes=preg)

                    def att_out(pr, hp=hp):
                        gb2 = 2 * pr
                        ob = new_bank()
                        for e_ in range(2):
                            h = 2 * hp + e_
                            vc = (0, 64, 192, 256)[h]
                            scur = 3 * e_ + pr % 3
                            sprev = 3 * e_ + (pr - 1) % 3
                            for j in range(2):
                                gb = gb2 + j
                                n = gb % nb
                                col = e_ * 256 + j * 128
                                pairs = []
                                rds = []
                                if n > 0:
                                    if j == 0:
                                        pairs.append((aT3.A((gb - 1) * 384 + vc, [[1, 128]]), pt_ap(sprev, 256 + 128, [[1, 128]])))
                                        rds += pt_regs(sprev)
                                    else:
                                        pairs.append((aT3.A((gb - 1) * 384 + vc, [[1, 128]]), pt_ap(scur, 128, [[1, 128]])))
                                    rds += aT3.R((gb - 1) * 384, 384)
                                pairs.append((aT3.A(gb * 384 + vc, [[1, 128]]), pt_ap(scur, j * 256, [[1, 128]])))
                                rds += aT3.R(gb * 384, 384) + pt_regs(scur)
                                mm_group(ob, (col, col + 128), pairs, reads=rds, last_inc=(e_ == 1 and j == 1))
                        g0 = gb2
                        src = AP(psum[ob][:].tensor, psum[ob][:].offset, [[512, P], [256, 2], [1, 256]])
                        if g == 0:
                            dst = aT1.A(2 * hp * 2048 + 128 * g0, [[2048, 2], [1, 256]])
                        elif g == 1:
                            dst = aT1.A(2 * hp * 2048 + 512 * (g0 % 4) + g0 // 4, [[2048, 2], [4, 256]])
                        else:
                            src = AP(psum[ob][:].tensor, psum[ob][:].offset, [[512, P], [256, 2], [128, 2], [1, 128]])
                            dst = aT1.A(2 * hp * 2048 + g0, [[2048, 2], [1, 2], [16, 128]])
                        aregs = aT1.R(2 * hp * 2048, 4096)
                        if g == 0:
                            k.op(ACT, lambda dst=dst, src=src: nc.scalar.copy(dst, src), reads=[ps_reg[ob]], writes=aregs)
                        else:
                            k.op(DVE, lambda dst=dst, src=src: nc.vector.tensor_tensor(dst, src, dst, ALU.add),
                                 reads=[ps_reg[ob]] + aregs, writes=aregs)

                    for pr in range(9):
                        if pr < 8:
                            att_scores(pr)
                        if pr >= 1:
                            att_out(pr - 1)

                chk('att%d' % g)
            for hp in range(2):
                j0, j1 = 2 * hp, 2 * hp + 1
                a0 = aT1.R(j0 * 2048, 2048)
                a1 = aT1.R(j1 * 2048, 2048)
                rr = aT4.R(0, 2048)
                for (jj, p0_, ar) in ((j0, 64, a0), (j1, 0, a1)):
                    dv = aT1.A(jj * 2048, [[1, S]], p0_, 64)
                    k.op(ACT, lambda dv=dv: nc.scalar.activation(dv, dv, AF.Ln), reads=ar, writes=ar)
                    k.op(ACT, lambda dv=dv: nc.scalar.activation(dv, dv, AF.Exp, scale=-1.0), reads=ar, writes=ar)
                k.op(DVE, lambda: nc.vector.tensor_copy(aT4.A(0, [[1, S]], 0, 64), aT1.A(j0 * 2048, [[1, S]], 64, 64)), reads=a0, writes=rr)
                k.op(DVE, lambda: nc.vector.tensor_copy(aT4.A(0, [[1, S]], 64, 64), aT1.A(j1 * 2048, [[1, S]], 0, 64)), reads=a1, writes=rr)
                k.op(DVE, lambda: nc.vector.tensor_tensor(attT[0:64, hp, :], aT1.A(j0 * 2048, [[1, S]], 0, 64), aT4.A(0, [[1, S]], 0, 64), ALU.mult),
                     reads=a0 + rr, writes=r_attT[hp])
                k.op(DVE, lambda: nc.vector.tensor_tensor(attT[64:128, hp, :], aT1.A(j1 * 2048, [[1, S]], 64, 64), aT4.A(0, [[1, S]], 64, 64), ALU.mult),
                     reads=a1 + rr, writes=r_attT[hp])
            if debug and b == 0:
                k.dma(SP, tl_dbg_for("attT"), [(dbg["attT"], attT[:])], reads=[r for rr_ in r_attT for r in rr_])

            chk('att')
            for hf in range(2):
                t0 = hf * HALF
                seq_start = (hf == 0)

                def hreg(kk, tt):
                    return r_hT[kk][2 * hf + tt]

                def p1_proj(pg_):
                    item_begin()
                    sp_ = wload(lambda s, pg_=pg_: [(wdst(s, KC, 0, 256), w_in_v[:, :, 2304 + 256 * pg_:2304 + 256 * pg_ + 256])])
                    ui = pg_ % 2
                    taus = list(range(1 if seq_start else 0, 9))
                    i = 0
                    while i < len(taus):
                        bk = new_bank()
                        grp = taus[i:i + 2]
                        for j, tau in enumerate(grp):
                            tk0 = t0 + 128 * (tau - 1)
                            tt_abs = tk0 // 512
                            mm_group(bk, (j * 256, j * 256 + 256),
                                     [(hT[:, kk, tk0:tk0 + 128], wv(sp_, kk, 0, 256)) for kk in range(KC)],
                                     reads=[r_w[sp_]] + [r_hT[kk][tt_abs] for kk in range(KC)], last_inc=(j == len(grp) - 1))
                        e = k.evac_eng()
                        dst = u_t[:, ui, grp[0]:grp[0] + len(grp), :]
                        src = psum[bk][:, 0:256 * len(grp)].rearrange("p (a c) -> p a c", c=256)
                        if e is ACT:
                            k.op(ACT, lambda dst=dst, src=src: nc.scalar.copy(dst, src), reads=[ps_reg[bk]], writes=[r_u[ui]])
                        else:
                            k.op(DVE, lambda dst=dst, src=src: nc.vector.tensor_copy(dst, src), reads=[ps_reg[bk]], writes=[r_u[ui]])
                        i += 2

                def p1_bands(pg_):
                    wpool = POOL_W[pg_]
                    ui = pg_ % 2
                    for cc in range(2):
                        pc = 2 * pg_ + cc
                        for tq in range(2):
                            bk = new_bank()
                            for s_ in range(4):
                                tau = 1 + 4 * tq + s_
                                first = seq_start and tau == 1
                                pairs = [(u_t[:, ui, tau, cc * 128:(cc + 1) * 128], bands_t[:, pg_ * 3 + (2 if first else 0), :])]
                                if not first:
                                    pairs.append((u_t[:, ui, tau - 1, cc * 128:(cc + 1) * 128], bands_t[:, pg_ * 3 + 1, :]))
                                mm_group(bk, (s_ * 128, s_ * 128 + 128), pairs, reads=[r_u[ui], r_bands], last_inc=(s_ == 3))
                            dst = aT2.A(pc * 1024 + tq * 512, [[1, 512]])
                            regs = aT2.R(pc * 1024 + tq * 512, 512)
                            e = k.evac_eng()
                            if e is ACT:
                                k.op(ACT, lambda dst=dst, bk=bk: nc.scalar.activation(dst, psum[bk][:, :], AF.Copy, scale=1.0 / wpool),
                                     reads=[ps_reg[bk]], writes=regs)
                            else:
                                k.op(DVE, lambda dst=dst, bk=bk: nc.vector.tensor_scalar(dst, psum[bk][:, :], 1.0 / wpool, None, ALU.mult),
                                     reads=[ps_reg[bk]], writes=regs)
                            if seq_start and tq == 0:
                                k.op(DVE, lambda bk=bk, pc=pc: nc.vector.tensor_tensor(
                                    aT2.A(pc * 1024, [[1, 16]]), psum[bk][:, 0:16], corr_t[:, pg_, :], ALU.mult),
                                    reads=[ps_reg[bk], r_const], writes=regs)
                for pg_ in range(4):
                    p1_proj(pg_)
                    if pg_ >= 1:
                        p1_bands(pg_ - 1)
                p1_bands(3)
                if debug and b == 0 and hf == 0:
                    k.dma(SP, tl_dbg_for("pmT"), [(dbg["pmT"], aT2.A(0, [[1024, KC], [1, HALF]]))], reads=aT2.R(0, 8192))

                chk('P1')
                item_begin()
                spg = wload(lambda s: [(wring[:, s, ci * 1024:(ci + 1) * 1024].rearrange("p (g o) -> p g o", g=4), w_pg_v[:, ci, :, :]) for ci in range(2)])
                for co in range(KC):
                    pg_, cc = co // 2, co % 2
                    for tt in range(2):
                        bk = new_bank()
                        pairs = [(wring[:, spg, ci * 1024 + pg_ * 256 + cc * 128: ci * 1024 + pg_ * 256 + cc * 128 + 128],
                                  aT2.A((2 * pg_ + ci) * 1024 + tt * 512, [[1, 512]])) for ci in range(2)]
                        rds = [r_w[spg]] + aT2.R((2 * pg_) * 1024 + tt * 512, 512) + aT2.R((2 * pg_ + 1) * 1024 + tt * 512, 512)
                        mm_group(bk, (0, 512), pairs, reads=rds)
                        dst = aT3.A(co * 1024 + tt * 512, [[1, 512]])
                        regs = aT3.R(co * 1024 + tt * 512, 512)
                        e = k.evac_eng()
                        if e is ACT:
                            k.op(ACT, lambda dst=dst, bk=bk, co=co: nc.scalar.activation(dst, psum[bk][:, :], AF.Copy, scale=prm[:, 4, co:co + 1]),
                                 reads=[ps_reg[bk], r_const], writes=regs)
                        else:
                            k.op(DVE, lambda dst=dst, bk=bk, co=co: nc.vector.tensor_scalar(dst, psum[bk][:, :], prm[:, 4, co:co + 1], None, ALU.mult),
                                 reads=[ps_reg[bk], r_const], writes=regs)
                if debug and b == 0 and hf == 0:
                    k.dma(SP, tl_dbg_for("pgT"), [(dbg["pgT"], aT3.A(0, [[1024, KC], [1, HALF]]))], reads=aT3.R(0, 8192))

                chk('P2')
                for m in range(KC):
                    yreg_all = aT1.R(m * 1024, 1024)
                    k.dma(SP, tl_y[m], [(aT1.A(m * 1024, [[1, HALF]]), xT[b, m * P:(m + 1) * P, t0:t0 + HALF])], writes=yreg_all)
                    k.op(ACT, lambda m=m: nc.scalar.activation(
                        aT1.A(m * 1024, [[1, HALF]]), aT1.A(m * 1024, [[1, HALF]]), AF.Copy, scale=ALPHA),
                        reads=yreg_all, writes=yreg_all)
                for m2 in range(4):
                    item_begin()
                    sga = wload(lambda s, m2=m2: [(wdst(s, KC, 0, 256), w_in_v[:, :, 3328 + 256 * m2:3328 + 256 * m2 + 256])])
                    sgb = wload(lambda s, m2=m2: [(wdst(s, KC, 0, 256), w_in_v[:, :, 4352 + 256 * m2:4352 + 256 * m2 + 256])])
                    sbp = wload(lambda s, m2=m2: [(wdst(s, KC, 0, 256), w_bp_v[:, :, 256 * m2:256 * m2 + 256])])
                    for mi in range(2):
                        m = 2 * m2 + mi
                        for tt in range(2):
                            tk = slice(t0 + tt * 512, t0 + tt * 512 + 512)
                            hrd = [hreg(kk, tt) for kk in range(KC)]
                            b_ga = new_bank()
                            mm_group(b_ga, (0, 512), [(wv(sga, kk, mi * 128, 128), hT[:, kk, tk]) for kk in range(KC)], reads=[r_w[sga]] + hrd)
                            b_gb = new_bank()
                            mm_group(b_gb, (0, 512), [(wv(sgb, kk, mi * 128, 128), hT[:, kk, tk]) for kk in range(KC)], reads=[r_w[sgb]] + hrd)
                            b_a = new_bank()
                            mm_group(b_a, (0, 512), [(watt[:, kc, m * 128:(m + 1) * 128], attT[:, kc, tk]) for kc in range(2)],
                                     reads=[r_watt] + [r_attT[kc][2 * hf + tt] for kc in range(2)])
                            b_b = new_bank()
                            mm_group(b_b, (0, 512), [(wv(sbp, kk, mi * 128, 128), aT3.A(kk * 1024 + tt * 512, [[1, 512]])) for kk in range(KC)],
                                     reads=[r_w[sbp]] + [r_ for kk in range(KC) for r_ in aT3.R(kk * 1024 + tt * 512, 512)])
                            si = 2 * (tt % 2)
                            k.op(ACT, lambda b_ga=b_ga, si=si: nc.scalar.activation(sg_t[:, si, :], psum[b_ga][:, :], AF.Sigmoid),
                                 reads=[ps_reg[b_ga]], writes=[r_sg[si]])
                            k.op(ACT, lambda b_gb=b_gb, si=si: nc.scalar.activation(sg_t[:, si + 1, :], psum[b_gb][:, :], AF.Sigmoid),
                                 reads=[ps_reg[b_gb]], writes=[r_sg[si + 1]])
                            k.op(DVE, lambda b_a=b_a, si=si: nc.vector.tensor_tensor(t12_t[:, 0, :], psum[b_a][:, :], sg_t[:, si, :], ALU.mult),
                                 reads=[ps_reg[b_a], r_sg[si]], writes=[r_t12[0]])
                            k.op(DVE, lambda b_b=b_b, si=si: nc.vector.tensor_tensor(t12_t[:, 1, :], psum[b_b][:, :], sg_t[:, si + 1, :], ALU.mult),
                                 reads=[ps_reg[b_b], r_sg[si + 1]], writes=[r_t12[1]])
                            mreg = aT2.R(m * 1024 + tt * 512, 512)
                            k.op(DVE, lambda m=m, tt=tt: nc.vector.tensor_tensor(
                                aT2.A(m * 1024 + tt * 512, [[1, 512]]), t12_t[:, 0, :], t12_t[:, 1, :], ALU.add),
                                reads=r_t12, writes=mreg)
                if debug and b == 0 and hf == 0:
                    k.dma(SP, tl_dbg_for("mrg"), [(dbg["mrg"], aT2.A(0, [[1024, KC], [1, HALF]]))], reads=aT2.R(0, 8192))

                chk('D1')
                stat = [[new_bank(hold=True) for _ in range(2)] for _ in range(2)]

                feed = {"n": 0, "q": []}

                def ln_stats_feed(m, tt, last):
                    yreg = aT1.R(m * 1024 + tt * 512, 512)
                    ysrc = aT1.A(m * 1024 + tt * 512, [[1, 512]])
                    i0 = 2 * (feed["n"] % 3)
                    feed["n"] += 1
                    k.op(ACT, lambda: nc.scalar.copy(ybs_t[:, i0, :], ysrc), reads=yreg, writes=[r_ybs[i0]])
                    k.op(ACT, lambda: nc.scalar.activation(ybs_t[:, i0 + 1, :], ysrc, AF.Square), reads=yreg, writes=[r_ybs[i0 + 1]])
                    feed["q"].append((m, tt, last, i0))
                    while len(feed["q"]) > 2:
                        ln_stats_pe(*feed["q"].pop(0))

                def ln_stats_flush():
                    while feed["q"]:
                        ln_stats_pe(*feed["q"].pop(0))

                def ln_stats_pe(m, tt, last, i0):
                    for j in range(2):
                        bk = stat[tt][j]
                        k.op(PE, lambda bk=bk, j=j: nc.tensor.matmul(psum[bk][:, :], onesm[:], ybs_t[:, i0 + j, :], start=(m == 0), stop=last),
                             reads=[r_ybs[i0 + j], r_const], writes=[ps_reg[bk]], inc=True)

                def ln_finish(tt):
                    bm, bq = stat[tt]
                    k.op(ACT, lambda: nc.scalar.activation(st_t[:, 0, :], psum[bm][:, :], AF.Square), reads=[ps_reg[bm]], writes=[r_st[0]])
                    k.op(DVE, lambda: nc.vector.tensor_tensor(st_t[:, 1, :], psum[bq][:, :], st_t[:, 0, :], ALU.subtract),
                         reads=[ps_reg[bq], r_st[0]], writes=[r_st[1]])
                    k.op(ACT, lambda: nc.scalar.activation(st_t[:, 1, :], st_t[:, 1, :], AF.Ln, bias=eps_t[:, 0:1]),
                         reads=[r_st[1], r_const], writes=[r_st[1]])
                    k.op(ACT, lambda: nc.scalar.activation(st_t[:, 2, :], st_t[:, 1, :], AF.Exp, scale=-0.5), reads=[r_st[1]], writes=[r_st[2]])
                    k.op(DVE, lambda: nc.vector.scalar_tensor_tensor(st_t[:, 3, :], psum[bm][:, :], -1.0, st_t[:, 2, :], ALU.mult, ALU.mult),
                         reads=[ps_reg[bm], r_st[2]], writes=[r_st[3]])
                    ps_held[bm] = False
                    ps_held[bq] = False

                def ln_norm(m, tt, ti):
                    yreg = aT1.R(m * 1024 + tt * 512, 512)
                    ysrc = aT1.A(m * 1024 + tt * 512, [[1, 512]])
                    k.op(DVE, lambda: nc.vector.tensor_tensor(t12_t[:, ti, :], ysrc, st_t[:, 2, :], ALU.mult),
                         reads=yreg + [r_st[2]], writes=[r_t12[ti]])
                    k.op(DVE, lambda: nc.vector.tensor_tensor(t12_t[:, ti, :], t12_t[:, ti, :], st_t[:, 3, :], ALU.add),
                         reads=[r_t12[ti], r_st[3]], writes=[r_t12[ti]])

                def ln1_apply_chunk(tt, m):
                    ti = m % 2
                    ln_norm(m, tt, ti)
                    yreg = aT1.R(m * 1024 + tt * 512, 512)
                    ydst = aT1.A(m * 1024 + tt * 512, [[1, 512]])
                    hreg2 = aT3.R(m * 1024 + tt * 512, 512)
                    k.op(ACT, lambda: nc.scalar.activation(
                        ydst, t12_t[:, ti, :], AF.Identity, bias=aGB[:, 1, m:m + 1], scale=aGB[:, 0, m:m + 1]),
                        reads=[r_t12[ti], r_m1], writes=yreg)
                    k.op(POOL, lambda: nc.gpsimd.tensor_scalar(
                        aT3.A(m * 1024 + tt * 512, [[1, 512]]), t12_t[:, ti, :], sc(G2, m, b), sc(B2, m, b), ALU.mult, ALU.add),
                        reads=[r_t12[ti], r_m2], writes=hreg2)

                item_begin()
                sos = [wload(lambda s, m2=m2: [(wdst(s, KC, 0, 256), w_out_v[:, :, 256 * m2:256 * m2 + 256])]) for m2 in range(4)]
                for tt in range(2):
                    for m in range(KC):
                        so, mi = sos[m // 2], m % 2
                        bk = new_bank()
                        mm_group(bk, (0, 512), [(wv(so, kk, mi * 128, 128), aT2.A(kk * 1024 + tt * 512, [[1, 512]])) for kk in range(KC)],
                                 reads=[r_w[so]] + [r_ for kk in range(KC) for r_ in aT2.R(kk * 1024 + tt * 512, 512)])
                        yreg = aT1.R(m * 1024 + tt * 512, 512)
                        ydst = aT1.A(m * 1024 + tt * 512, [[1, 512]])
                        k.op(DVE, lambda bk=bk, ydst=ydst, m=m: nc.vector.scalar_tensor_tensor(
                            ydst, psum[bk][:, :], sc(modT, 16 + m, b), ydst, ALU.mult, ALU.add),
                            reads=[ps_reg[bk], r_m2] + yreg, writes=yreg)
                        ln_stats_feed(m, tt, last=(m == KC - 1))
                        if tt == 1:
                            if m == 1:
                                ln_finish(0)
                            if m >= 2:
                                ln1_apply_chunk(0, m - 2)
                ln1_apply_chunk(0, KC - 2)
                ln1_apply_chunk(0, KC - 1)
                ln_stats_flush()
                ln_finish(1)
                for m in range(KC):
                    ln1_apply_chunk(1, m)
                if debug and b == 0 and hf == 0:
                    k.dma(SP, tl_dbg_for("y1h2"), [(dbg["y1"], aT1.A(0, [[1024, KC], [1, HALF]])), (dbg["h2"], aT3.A(0, [[1024, KC], [1, HALF]]))],
                          reads=aT1.R(0, 8192) + aT3.R(0, 8192))

                if hf == 1 and b + 1 < NB:
                    emit_S0(b + 1, early=True)
                chk('D2')
                for fgi, (f0, f1) in enumerate(FGROUPS):
                    nf = f1 - f0
                    fl = 0
                    while fl < nf:
                        nch = min(4 if (fgi == 0 and fl == 0) else 2, nf - fl)
                        item_begin()
                        sl_g, sl_u = [], []
                        for c_ in range(0, nch, 2):
                            ncols = min(2, nch - c_) * 128
                            c0 = (f0 + fl + c_) * 128
                            sl_g.append(wload(lambda s, c0=c0, ncols=ncols: [(wdst(s, KC, 0, ncols), w_gate_v[:, :, c0:c0 + ncols])]))
                            sl_u.append(wload(lambda s, c0=c0, ncols=ncols: [(wdst(s, KC, 0, ncols), w_up_v[:, :, c0:c0 + ncols])]))
                        for tt in range(2):
                            for fi in range(nch):
                                sgt, sup, fo = sl_g[fi // 2], sl_u[fi // 2], (fi % 2) * 128
                                h2r = [r_ for kk in range(KC) for r_ in aT3.R(kk * 1024 + tt * 512, 512)]
                                bg = new_bank()
                                mm_group(bg, (0, 512), [(wv(sgt, kk, fo, 128), aT3.A(kk * 1024 + tt * 512, [[1, 512]])) for kk in range(KC)],
                                         reads=[r_w[sgt]] + h2r)
                                bu = new_bank()
                                mm_group(bu, (0, 512), [(wv(sup, kk, fo, 128), aT3.A(kk * 1024 + tt * 512, [[1, 512]])) for kk in range(KC)],
                                         reads=[r_w[sup]] + h2r)
                                si = (2 * fi + tt) % 4
                                k.op(ACT, lambda bg=bg, si=si: nc.scalar.activation(sg_t[:, si, :], psum[bg][:, :], AF.Silu),
                                     reads=[ps_reg[bg]], writes=[r_sg[si]])
                                areg = aT4.R(((fl + fi) * 1024 + tt * 512) // 2, 256)
                                k.op(DVE, lambda bu=bu, si=si, fl=fl, fi=fi, tt=tt: nc.vector.tensor_tensor(
                                    actT_ap(fl + fi, tt * 512, 512), psum[bu][:, :], sg_t[:, si, :], ALU.mult),
                                    reads=[ps_reg[bu], r_sg[si]], writes=areg)
                        fl += nch
                    last_group = (fgi == len(FGROUPS) - 1)

                    def dn_step(sd, mi, m, tt, nf=nf):
                        bk = new_bank()
                        mm_group(bk, (0, 512), [(wv(sd, kf, mi * 128, 128), actT_ap(kf, tt * 512, 512)) for kf in range(nf)],
                                 reads=[r_w[sd]] + [r_ for kf in range(nf) for r_ in aT4.R((kf * 1024 + tt * 512) // 2, 256)])
                        yreg = aT1.R(m * 1024 + tt * 512, 512)
                        ydst = aT1.A(m * 1024 + tt * 512, [[1, 512]])
                        k.op(DVE, lambda: nc.vector.scalar_tensor_tensor(
                            ydst, psum[bk][:, :], sc(modT, 40 + m, b), ydst, ALU.mult, ALU.add),
                            reads=[ps_reg[bk], r_m2] + yreg, writes=yreg)

                    if not last_group:
                        for m2 in range(4):
                            item_begin()
                            sd = wload(lambda s, nf=nf, f0=f0, f1=f1, m2=m2: [(wdst(s, nf, 0, 256), w_down_v[:, f0:f1, 256 * m2:256 * m2 + 256])])
                            for mi in range(2):
                                for tt in range(2):
                                    dn_step(sd, mi, 2 * m2 + mi, tt)
                    else:
                        stat = [[new_bank(hold=True) for _ in range(2)] for _ in range(2)]
                        oi_ = [0]

                        def ln2_apply_chunk(tt, m):
                            ti = m % 2
                            ln_norm(m, tt, ti)
                            oi_[0] = (oi_[0] + 1) % 2
                            oi = oi_[0]
                            k.op(ACT, lambda: nc.scalar.activation(
                                ost_t[:, oi, :], t12_t[:, ti, :], AF.Identity, bias=prm[:, 3, m:m + 1], scale=prm[:, 2, m:m + 1]),
                                reads=[r_t12[ti], r_const], writes=[r_ost[oi]])
                            k.dma(SP, tl_o[oi], [(outT[b, m * P:(m + 1) * P, t0 + tt * 512:t0 + tt * 512 + 512], ost_t[:, oi, :])],
                                  reads=[r_ost[oi]])

                        item_begin()
                        sds = [wload(lambda s, nf=nf, f0=f0, f1=f1, m2=m2: [(wdst(s, nf, 0, 256), w_down_v[:, f0:f1, 256 * m2:256 * m2 + 256])])
                               for m2 in range(4)]
                        for tt in range(2):
                            for m in range(KC):
                                dn_step(sds[m // 2], m % 2, m, tt)
                                ln_stats_feed(m, tt, last=(m == KC - 1))
                                if tt == 1:
                                    if m == 1:
                                        ln_finish(0)
                                    if m >= 2:
                                        ln2_apply_chunk(0, m - 2)
                        ln2_apply_chunk(0, KC - 2)
                        ln2_apply_chunk(0, KC - 1)
                        chk('FFN')
                        ln_stats_flush()
                        ln_finish(1)
                        for m in range(KC):
                            ln2_apply_chunk(1, m)

        def emit_all():
            try:
                emit_consts()
                for it_ in range(8):
                    emit_mod(it_)
                emit_derived1()
                chk('mod')
                for b_ in range(NB):
                    seq_body(b_)
            except _Stop:
                pass

        k.dry = True
        emit_all()
        k.dry = False
        ps_next[0] = 0
        for i_ in range(8):
            ps_held[i_] = False
        k.flip = 0
        s0_done.clear()
        emit_all()

        for tl in tl_o + list(tl_dbgs.values()):
            if tl.cnt:
                nc.sync.wait_ge(tl.sem, tl.cnt)
    return nc


_NC_CACHE = {}


def _get_nc(debug=False):
    if debug not in _NC_CACHE:
        _NC_CACHE[debug] = build_program(debug)
    return _NC_CACHE[debug]


def make_in_maps(inputs):
    f = lambda a: np.ascontiguousarray(np.asarray(a, dtype=np.float32))
    x = f(inputs["x"])
    c = f(inputs["c"])
    vecT = lambda v, n: f(np.asarray(v, np.float32).reshape(n, P).T)
    bias, bands, corr = const_tables()
    shared = {
        "w_ada": f(inputs["w_ada"][0]), "b_adaT": vecT(inputs["b_ada"][0], 48),
        "w_in": f(inputs["w_in"][0]), "w_batt": f(inputs["w_branch_att"][0]),
        "w_pg": f(inputs["w_pool_group"][0]), "pscaleT": vecT(inputs["pool_scale"][0], KC),
        "w_bp": f(inputs["w_branch_pool"][0]), "w_out": f(inputs["w_out"][0]),
        "ln1gT": vecT(inputs["ln1_g"][0], KC), "ln1bT": vecT(inputs["ln1_b"][0], KC),
        "ln2gT": vecT(inputs["ln2_g"][0], KC), "ln2bT": vecT(inputs["ln2_b"][0], KC),
        "w_gate": f(inputs["w_gate"][0]), "w_up": f(inputs["w_up"][0]), "w_down": f(inputs["w_down"][0]),
        "bias_tab": bias, "bands": bands, "corr": corr,
    }
    in_maps = []
    for i in range(N_CORES):
        xs = x[NB * i:NB * (i + 1)]
        m = dict(shared)
        m["xT"] = f(xs.transpose(0, 2, 1))
        cs = c[NB * i:NB * (i + 1)]
        m["cT"] = f(cs.reshape(NB, KC, P).transpose(2, 1, 0))
        in_maps.append(m)
    return in_maps


def kernel(**inputs):
    nc = _get_nc(False)
    in_maps = make_in_maps(inputs)
    res = run_bass_kernel_spmd(nc, in_maps, core_ids=list(range(N_CORES)))
    out = np.empty((NB * N_CORES, S, D), np.float32)
    for i in range(N_CORES):
        out[NB * i:NB * (i + 1)] = np.asarray(res.results[i]["outT"]).transpose(0, 2, 1)
    return out
```

```python
import contextlib
import math

import numpy as np

import concourse.bass as bass
import concourse.mybir as mybir
from concourse.ap import AP
from concourse.bass_utils import run_bass_kernel_spmd

F32 = mybir.dt.float32
BF16 = mybir.dt.bfloat16
ALU = mybir.AluOpType
AF = mybir.ActivationFunctionType

P = 128
D = 1024
KC = 8
S = 2048
HALF = 1024
NB = 2
N_CORES = 8
IN_W = 5376
DFF = 2816
NFC = 22
ALPHA = 2.0 ** 0.25
LN_EPS = 1e-5
DIL = (1, 4, 16)
NBLK = (16, 4, 1)
POOL_W = (2, 4, 8, 16)
NW = 6
FGROUPS = ((0, 8), (8, 15), (15, 22))


def alibi_slopes_np(n):
    def pow2(m):
        start = 2.0 ** (-8.0 / m)
        return [start ** (i + 1) for i in range(m)]
    if math.log2(n).is_integer():
        s = pow2(n)
    else:
        c = 2 ** math.floor(math.log2(n))
        s = pow2(c) + pow2(2 * c)[0::2][: n - c]
    return np.array(sorted(s, reverse=True), dtype=np.float32)


def const_tables():
    slopes = alibi_slopes_np(12).reshape(3, 4)
    k = np.arange(128)[:, None]
    c = np.arange(256)[None, :]
    diff = np.where(c < 128, c - k, 128 + (c - 128) - k)
    valid = (diff >= 0) & (diff <= 128)
    bias = np.empty((128, 12, 256), np.float32)
    for g in range(3):
        for h in range(4):
            b = -(slopes[g, h] * (diff * DIL[g]).astype(np.float32)).astype(np.float32)
            bias[:, g * 4 + h, :] = np.where(valid, b, np.float32(-30000.0))
    tp = np.arange(128)[:, None]
    t = np.arange(128)[None, :]
    bands = np.zeros((128, 12, 128), np.float32)
    corr = np.zeros((128, 4, 16), np.float32)
    for g, w in enumerate(POOL_W):
        inwin = ((t - tp) >= 0) & ((t - tp) < w)
        eye = (t == tp)
        bands[:, g * 3 + 0, :] = inwin.astype(np.float32) - w * eye
        prev = ((t + 128 - tp) >= 0) & ((t + 128 - tp) < w)
        bands[:, g * 3 + 1, :] = prev.astype(np.float32)
        cnt = np.minimum(t + 1, w).astype(np.float32)
        bands[:, g * 3 + 2, :] = inwin.astype(np.float32) - cnt * eye
        corr[:, g, :] = (1.0 / np.minimum(np.arange(16) + 1, w)).astype(np.float32)[None, :]
    return bias, bands, corr


class Tl:
    def __init__(self, sem):
        self.sem = sem
        self.cnt = 0


class Reg:
    __slots__ = ("w", "r")

    def __init__(self):
        self.w = None
        self.r = {}


class Eng:
    def __init__(self, eng, sem, is_pe=False):
        self.eng = eng
        self.tl = Tl(sem)
        self.seen = {}
        self.is_pe = is_pe


class Arena:
    def __init__(self, t, n_elems, gran):
        self.ap = t[:]
        self.tensor = self.ap.tensor
        self.pitch = self.ap.ap[0][0]
        self.gran = gran
        self.regs = [Reg() for _ in range((n_elems + gran - 1) // gran)]

    def R(self, off, n):
        return self.regs[off // self.gran:(off + n - 1) // self.gran + 1]

    def A(self, off, dims, p0=0, np_=P):
        return AP(self.tensor, p0 * self.pitch + off, [[self.pitch, np_]] + [list(d) for d in dims])


class K:
    def __init__(self, nc, es):
        self.nc = nc
        self.es = es
        self.nsem = 0
        self.pe = Eng(nc.tensor, self.sem("tl_pe"), True)
        self.act = Eng(nc.scalar, self.sem("tl_act"))
        self.dve = Eng(nc.vector, self.sem("tl_dve"))
        self.pool = Eng(nc.gpsimd, self.sem("tl_pool"))
        self.sp = Eng(nc.sync, self.sem("tl_sp"))
        self.flip = 0
        self.dry = False

    def sem(self, name):
        self.nsem += 1
        return self.es.enter_context(self.nc.semaphore(name))

    def sb(self, name, shape, dt):
        return self.es.enter_context(self.nc.sbuf_tensor(name, shape, dt))

    def _waits(self, e, reads, writes):
        deps = {}

        def add(tok, raw):
            if tok is None:
                return
            tl, v = tok
            if tl is e.tl and e.is_pe:
                return
            if deps.get(tl, 0) < v:
                deps[tl] = v
        for r in reads:
            add(r.w, True)
        for w in writes:
            add(w.w, False)
            for tok in w.r.values():
                add(tok, False)
        for tl, v in deps.items():
            if e.seen.get(tl, 0) < v:
                e.eng.wait_ge(tl.sem, v)
                e.seen[tl] = v

    def op(self, e, fn, reads=(), writes=(), inc=True):
        if self.dry:
            return None
        self._waits(e, reads, writes)
        tick = e.tl.cnt + 1
        ins = fn()
        if inc:
            ins.then_inc(e.tl.sem, 1)
            e.tl.cnt = tick
        tok = (e.tl, tick)
        for r in reads:
            r.r[e.tl] = tok
        for w in writes:
            w.w = tok
            w.r = {}
        return ins

    def dma(self, q, tl, xfers, reads=(), writes=()):
        if self.dry:
            return
        self._waits(q, reads, writes)
        if tl.cnt and q.seen.get(tl, 0) < tl.cnt:
            q.eng.wait_ge(tl.sem, tl.cnt)
            q.seen[tl] = tl.cnt
        for (o, i) in xfers:
            q.eng.dma_start(out=o, in_=i).then_inc(tl.sem, 16)
            tl.cnt += 16
        tok = (tl, tl.cnt)
        for r in reads:
            r.r[tl] = tok
        for w in writes:
            w.w = tok
            w.r = {}

    def evac_eng(self):
        self.flip ^= 1
        return self.act if self.flip else self.dve


class _Stop(Exception):
    pass


def build_program(debug=False, stop_after=None):
    nc = bass.Bass("TRN2", target_bir_lowering=False)

    def chk(name):
        if stop_after == name:
            raise _Stop()

    def din(name, shape):
        return nc.dram_tensor(name, shape, F32, kind="ExternalInput").ap()
    xT = din("xT", [NB, D, S])
    cT = din("cT", [P, KC, NB])
    w_ada = din("w_ada", [D, 6 * D])
    b_adaT = din("b_adaT", [P, 48])
    w_in = din("w_in", [D, IN_W])
    w_batt = din("w_batt", [256, D])
    w_pg = din("w_pg", [4, 256, 256])
    pscaleT = din("pscaleT", [P, KC])
    w_bp = din("w_bp", [D, D])
    w_out = din("w_out", [D, D])
    ln1gT = din("ln1gT", [P, KC])
    ln1bT = din("ln1bT", [P, KC])
    ln2gT = din("ln2gT", [P, KC])
    ln2bT = din("ln2bT", [P, KC])
    w_gate = din("w_gate", [D, DFF])
    w_up = din("w_up", [D, DFF])
    w_down = din("w_down", [DFF, D])
    bias_d = din("bias_tab", [P, 12, 256])
    bands_d = din("bands", [P, 12, 128])
    corr_d = din("corr", [P, 4, 16])
    outT = nc.dram_tensor("outT", [NB, D, S], F32, kind="ExternalOutput").ap()
    dbg = {}
    if debug:
        dbg["modT"] = nc.dram_tensor("dbg_modT", [P, 48, NB], F32, kind="ExternalOutput").ap()
        dbg["hT"] = nc.dram_tensor("dbg_hT", [P, KC, S], BF16, kind="ExternalOutput").ap()
        dbg["attT"] = nc.dram_tensor("dbg_attT", [P, 2, S], BF16, kind="ExternalOutput").ap()
        dbg["pmT"] = nc.dram_tensor("dbg_pmT", [P, KC, HALF], BF16, kind="ExternalOutput").ap()
        dbg["pgT"] = nc.dram_tensor("dbg_pgT", [P, KC, HALF], BF16, kind="ExternalOutput").ap()
        dbg["mrg"] = nc.dram_tensor("dbg_mrg", [P, KC, HALF], BF16, kind="ExternalOutput").ap()
        dbg["y1"] = nc.dram_tensor("dbg_y1", [P, KC, HALF], F32, kind="ExternalOutput").ap()
        dbg["h2"] = nc.dram_tensor("dbg_h2", [P, KC, HALF], BF16, kind="ExternalOutput").ap()

    w_in_v = w_in.rearrange("(k p) n -> p k n", p=P)
    w_ada_v = w_ada.rearrange("(k p) n -> p k n", p=P)
    w_bp_v = w_bp.rearrange("(k p) n -> p k n", p=P)
    w_out_v = w_out.rearrange("(k p) n -> p k n", p=P)
    w_gate_v = w_gate.rearrange("(k p) n -> p k n", p=P)
    w_up_v = w_up.rearrange("(k p) n -> p k n", p=P)
    w_down_v = w_down.rearrange("(k p) n -> p k n", p=P)
    w_batt_v = w_batt.rearrange("(k p) n -> p k n", p=P)
    w_pg_v = w_pg.rearrange("g (c p) o -> p c g o", p=P)

    with contextlib.ExitStack() as es:
        k = K(nc, es)
        PE, ACT, DVE, POOL, SP = k.pe, k.act, k.dve, k.pool, k.sp

        bias_t = k.sb("bias_t", [P, 12, 256], F32)
        bands_t = k.sb("bands_t", [P, 12, 128], BF16)
        corr_t = k.sb("corr_t", [P, 4, 16], F32)
        onesm = k.sb("onesm", [P, P], BF16)
        cT_t = k.sb("cT_t", [P, KC, NB], F32)
        scT_t = k.sb("scT_t", [P, KC, NB], BF16)
        badaT_t = k.sb("badaT_t", [P, 48], F32)
        modT = k.sb("modT", [P, 48, NB], F32)
        prm = k.sb("prm", [P, 5, KC], F32)
        A1 = k.sb("A1", [P, KC, NB], F32)
        A2 = k.sb("A2", [P, KC, NB], F32)
        G2 = k.sb("G2", [P, KC, NB], F32)
        B2 = k.sb("B2", [P, KC, NB], F32)
        aGB = k.sb("aGB", [P, 2, KC], F32)
        watt = k.sb("watt", [P, 2, D], BF16)
        hT = k.sb("hT", [P, KC, S], BF16)
        attT = k.sb("attT", [P, 2, S], BF16)
        wring = k.sb("wring", [P, NW, KC * 256], BF16)
        T1 = k.sb("T1", [P, 8192], F32)
        T2 = k.sb("T2", [P, 8192], BF16)
        T3 = k.sb("T3", [P, 8192], BF16)
        T4 = k.sb("T4", [P, 4096], F32)
        u_t = k.sb("u_t", [P, 2, 9, 256], BF16)
        sg_t = k.sb("sg_t", [P, 4, 512], F32)
        t12_t = k.sb("t12_t", [P, 2, 512], F32)
        ybs_t = k.sb("ybs_t", [P, 6, 512], BF16)
        st_t = k.sb("st_t", [P, 4, 512], F32)
        ost_t = k.sb("ost_t", [P, 2, 512], F32)
        eps_t = k.sb("eps_t", [P, 1], F32)

        psum = [es.enter_context(nc.psum_tensor(f"ps{i}", [P, 512], F32)) for i in range(8)]
        ps_reg = [Reg() for _ in range(8)]
        ps_held = [False] * 8
        ps_next = [0]

        def new_bank(hold=False):
            for _ in range(16):
                i = ps_next[0]
                ps_next[0] = (i + 1) % 8
                if not ps_held[i]:
                    ps_held[i] = hold
                    return i
            raise RuntimeError("no psum bank")

        aT1 = Arena(T1, 8192, 512)
        aT2 = Arena(T2, 8192, 512)
        aT3 = Arena(T3, 8192, 512)
        aT4 = Arena(T4, 4096, 256)
        T4b = T4[:].bitcast(BF16)
        T4b_pitch = T4b.ap[0][0]

        def actT_ap(fl, off, n):
            return AP(T4b.tensor, T4b.offset + fl * 1024 + off, [[T4b_pitch, P], [1, n]])

        r_hT = [[Reg() for _ in range(4)] for _ in range(KC)]
        r_attT = [[Reg() for _ in range(4)] for _ in range(2)]
        r_w = [Reg() for _ in range(NW)]
        r_const = Reg()
        r_m1 = Reg()
        r_m2 = Reg()
        r_u = [Reg(), Reg()]
        r_sg = [Reg() for _ in range(4)]
        r_t12 = [Reg(), Reg()]
        r_ybs = [Reg() for _ in range(6)]
        r_st = [Reg() for _ in range(4)]
        r_ost = [Reg() for _ in range(2)]
        r_watt = Reg()
        r_bands = Reg()

        tl_w = [Tl(k.sem(f"dw{i}")) for i in range(NW)]
        tl_c = Tl(k.sem("dconst"))
        tl_c2 = Tl(k.sem("dconst2"))
        tl_x = [Tl(k.sem(f"dx{i}")) for i in range(4)]
        tl_o = [Tl(k.sem(f"do{i}")) for i in range(2)]
        tl_dbgs = {}

        def tl_dbg_for(name):
            if k.dry:
                return None
            tl_dbgs[name] = Tl(k.sem("ddbg_" + name))
            return tl_dbgs[name]
        tl_y = [Tl(k.sem(f"dy{i}")) for i in range(8)]

        w_next = [0]

        def wslot():
            i = w_next[0]
            w_next[0] = (i + 1) % NW
            return i

        def wv(slot, kk, c0, n):
            return wring[:, slot, kk * 256 + c0: kk * 256 + c0 + n]

        plan = []
        wst = {"req": 0, "emitted": 0}

        def item_begin():
            if k.dry:
                return
            hi = min(len(plan), wst["req"] + NW)
            while wst["emitted"] < hi:
                j = wst["emitted"]
                s = j % NW
                k.dma(POOL, tl_w[s], plan[j](s), writes=[r_w[s]])
                wst["emitted"] += 1

        def wload(xfers_fn):
            if k.dry:
                plan.append(xfers_fn)
                return (len(plan) - 1) % NW
            j = wst["req"]
            assert j < wst["emitted"], "wload without item_begin"
            wst["req"] += 1
            return j % NW

        def wdst(slot, nk, c0, n):
            return wring[:, slot, :].rearrange("p (k c) -> p k c", c=256)[:, 0:nk, c0:c0 + n]

        def emit_consts():
            k.dma(SP, tl_c, [
                (bias_t[:], bias_d), (corr_t[:], corr_d), (cT_t[:], cT), (badaT_t[:], b_adaT),
                (prm[:, 0, :], ln1gT), (prm[:, 1, :], ln1bT), (prm[:, 2, :], ln2gT), (prm[:, 3, :], ln2bT),
                (prm[:, 4, :], pscaleT),
            ], writes=[r_const])
            k.dma(POOL, tl_c2, [(bands_t[:], bands_d), (watt[:], w_batt_v)], writes=[r_bands, r_watt])
            k.op(DVE, lambda: nc.vector.memset(onesm[:], 1.0 / D), writes=[r_const])
            k.op(DVE, lambda: nc.vector.memset(eps_t[:], LN_EPS), writes=[r_const])
            k.op(ACT, lambda: nc.scalar.activation(scT_t[:], cT_t[:], AF.Silu), reads=[r_const], writes=[r_m1])
            k.op(DVE, lambda: nc.vector.tensor_scalar(aGB[:], prm[:, 0:2, :], ALPHA, None, ALU.mult), reads=[r_const], writes=[r_m1])

        def emit_mod(it):
            rm = r_m1 if it < 8 else r_m2
            item_begin()
            sl = wload(lambda s: [(wdst(s, KC, 0, 256), w_ada_v[:, :, 256 * it:256 * it + 256])])
            bk = new_bank()
            for mc in range(2):
                for kk in range(KC):
                    k.op(PE, lambda kk=kk, mc=mc: nc.tensor.matmul(
                        psum[bk][:, mc * 2:mc * 2 + 2], wv(sl, kk, mc * 128, 128), scT_t[:, kk, :], start=(kk == 0), stop=(kk == KC - 1)),
                        reads=[r_w[sl], r_m1], writes=[ps_reg[bk]], inc=(mc == 1 and kk == KC - 1))
            j0 = 2 * it
            k.op(DVE, lambda: nc.vector.tensor_tensor(
                modT[:, j0:j0 + 2, :], psum[bk][:, 0:4].rearrange("p (j b) -> p j b", b=NB),
                badaT_t[:, j0:j0 + 2].unsqueeze(2).broadcast_to([P, 2, NB]), ALU.add),
                reads=[ps_reg[bk], r_const], writes=[rm])

        def emit_derived1():
            k.op(DVE, lambda: nc.vector.tensor_scalar(A1[:], modT[:, 8:16, :], 1.0, None, ALU.add), reads=[r_m1], writes=[r_m1])

        def emit_derived2():
            k.op(DVE, lambda: nc.vector.tensor_scalar(A2[:], modT[:, 32:40, :], 1.0, None, ALU.add), reads=[r_m2], writes=[r_m2])
            k.op(DVE, lambda: nc.vector.tensor_tensor(
                G2[:], A2[:], prm[:, 0, :].unsqueeze(2).broadcast_to([P, KC, NB]), ALU.mult), reads=[r_m2, r_const], writes=[r_m2])
            k.op(DVE, lambda: nc.vector.tensor_tensor(
                B2[:], A2[:], prm[:, 1, :].unsqueeze(2).broadcast_to([P, KC, NB]), ALU.mult), reads=[r_m2, r_const], writes=[r_m2])
            k.op(DVE, lambda: nc.vector.tensor_tensor(B2[:], B2[:], modT[:, 24:32, :], ALU.add), reads=[r_m2], writes=[r_m2])
            if debug:
                k.dma(SP, tl_dbg_for("modT"), [(dbg["modT"], modT[:])], reads=[r_m1, r_m2])

        def sc(t, m, b):
            return t[:, m, b:b + 1]

        def mm_group(bank, cols, pairs, reads, last_inc=True):
            n = len(pairs)
            for i, (lhsT, rhs) in enumerate(pairs):
                k.op(PE, lambda lhsT=lhsT, rhs=rhs, i=i: nc.tensor.matmul(
                    psum[bank][:, cols[0]:cols[1]], lhsT, rhs, start=(i == 0), stop=(i == n - 1)),
                    reads=reads, writes=[ps_reg[bank]], inc=(last_inc and i == n - 1))

        def tok_slice(g, gb):
            d, nb = DIL[g], NBLK[g]
            r, n = gb // nb, gb % nb
            start = r + d * 128 * n
            return slice(start, start + d * 127 + 1, d)

        T2f = T2[:].bitcast(F32)
        T2f_pitch = T2f.ap[0][0]
        s0_done = set()

        def emit_S0(b, early=False):
            if b in s0_done:
                return
            s0_done.add(b)
            for m in range(KC):
                if early:
                    q = m % 2
                    regs = aT2.R(q * 4096, 4096)
                    stg = AP(T2f.tensor, T2f.offset + q * 2048, [[T2f_pitch, P], [1, S]])
                else:
                    q = m % 4
                    regs = aT1.R(q * 2048, 2048)
                    stg = aT1.A(q * 2048, [[1, S]])
                k.dma(SP, tl_x[q], [(stg, xT[b, m * P:(m + 1) * P, :])], writes=regs)
                if m % 2 == 0:
                    k.op(ACT, lambda m=m, stg=stg: nc.scalar.activation(
                        hT[:, m, :], stg, AF.Identity, bias=sc(modT, m, b), scale=sc(A1, m, b)),
                        reads=regs + [r_m1], writes=r_hT[m])
                else:
                    k.op(DVE, lambda m=m, stg=stg: nc.vector.tensor_scalar(
                        hT[:, m, :], stg, sc(A1, m, b), sc(modT, m, b), ALU.mult, ALU.add),
                        reads=regs + [r_m1], writes=r_hT[m])

        def seq_body(b):
            emit_S0(b)
            if debug and b == 0:
                k.dma(SP, tl_dbg_for("hT"), [(dbg["hT"], hT[:])], reads=[r for rr in r_hT for r in rr])

            chk('S0')
            k.op(DVE, lambda: nc.vector.memset(aT3.A(64, [[384, 16], [192, 2], [1, 64]]), 1.0), writes=aT3.R(0, 6144))

            for g in range(3):
                d, nb = DIL[g], NBLK[g]
                item_begin()
                sq = wload(lambda s, g=g: [(wdst(s, KC, 0, 256), w_in_v[:, :, 256 * g:256 * g + 256])])
                sk = wload(lambda s, g=g: [(wdst(s, KC, 0, 256), w_in_v[:, :, 768 + 256 * g:768 + 256 * g + 256])])
                for mc in range(4):
                    slot = sq if mc < 2 else sk
                    c0 = (mc % 2) * 128
                    for tt in range(4):
                        bk = new_bank()
                        mm_group(bk, (0, 512), [(wv(slot, kk, c0, 128), hT[:, kk, tt * 512:(tt + 1) * 512]) for kk in range(KC)],
                                 reads=[r_w[slot]] + [r_hT[kk][tt] for kk in range(KC)])
                        if g == 0:
                            src = psum[bk][:, :]
                            dst = aT2.A(mc * 2048 + 512 * tt, [[1, 512]])
                        elif g == 1:
                            src = AP(psum[bk][:].tensor, psum[bk][:].offset, [[512, P], [1, 4], [4, 128]])
                            dst = aT2.A(mc * 2048 + 128 * tt, [[512, 4], [1, 128]])
                        else:
                            src = AP(psum[bk][:].tensor, psum[bk][:].offset, [[512, P], [1, 16], [16, 32]])
                            dst = aT2.A(mc * 2048 + 32 * tt, [[128, 16], [1, 32]])
                        regs = aT2.R(mc * 2048, 2048)
                        scl = 0.125 if mc < 2 else 1.0
                        e = k.evac_eng()
                        if e is ACT:
                            k.op(ACT, lambda dst=dst, src=src, scl=scl: nc.scalar.activation(dst, src, AF.Copy, scale=scl),
                                 reads=[ps_reg[bk]], writes=regs)
                        else:
                            k.op(DVE, lambda dst=dst, src=src, scl=scl: nc.vector.tensor_scalar(dst, src, scl, None, ALU.mult),
                                 reads=[ps_reg[bk]], writes=regs)
                chk('qk%d' % g)
                chk('v%d' % g)
                if b == 0 and g < 2:
                    for it_ in range(8 + 8 * g, 16 + 8 * g):
                        emit_mod(it_)
                    if g == 1:
                        emit_derived2()
                item_begin()
                sv = wload(lambda s, g=g: [(wdst(s, KC, 0, 256), w_in_v[:, :, 1536 + 256 * g:1536 + 256 * g + 256])])
                def vproj(gb, g=g, sv=sv):
                    bk = new_bank()
                    for j in range(2):
                        ts = tok_slice(g, gb + j)
                        tts = sorted(set([ts.start // 512, (ts.stop - 1) // 512])) if g == 0 else range(4)
                        mm_group(bk, (j * 256, j * 256 + 256), [(hT[:, kk, ts], wv(sv, kk, 0, 256)) for kk in range(KC)],
                                 reads=[r_w[sv]] + [r_hT[kk][t_] for kk in range(KC) for t_ in tts], last_inc=(j == 1))
                    for j in range(2):
                        src = AP(psum[bk][:].tensor, psum[bk][:].offset + j * 256, [[512, P], [128, 2], [64, 2], [1, 64]])
                        dst = aT3.A((gb + j) * 384, [[192, 2], [128, 2], [1, 64]])
                        e = k.evac_eng()
                        regs = aT3.R((gb + j) * 384, 384)
                        if e is ACT:
                            k.op(ACT, lambda dst=dst, src=src: nc.scalar.copy(dst, src), reads=[ps_reg[bk]], writes=regs)
                        else:
                            k.op(DVE, lambda dst=dst, src=src: nc.vector.tensor_copy(dst, src), reads=[ps_reg[bk]], writes=regs)

                def pt_ap(sidx, rel, dims, np_=P):
                    if sidx < 4:
                        return aT3.A(6144 + sidx * 512 + rel, dims)
                    return AP(T4b.tensor, T4b.offset + 6144 + (sidx - 4) * 512 + rel, [[T4b_pitch, np_]] + [list(d_) for d_ in dims])

                def pt_regs(sidx):
                    if sidx < 4:
                        return aT3.R(6144 + sidx * 512, 512)
                    return aT4.R(3072 + (sidx - 4) * 256, 256)

                for hp in range(2):
                    qreg = aT2.R(hp * 2048, 2048)
                    kreg = aT2.R((2 + hp) * 2048, 2048)

                    def att_scores(pr, hp=hp, qreg=qreg, kreg=kreg):
                        gb2 = 2 * pr
                        ws = [256 if ((gb2 + j) % nb) < nb - 1 else 128 for j in range(2)]
                        sbk = [new_bank(), new_bank()]
                        for j in range(2):
                            gb = gb2 + j
                            for e_ in range(2):
                                kT = aT2.A((2 + hp) * 2048 + gb * 128, [[1, 128]], p0=64 * e_, np_=64)
                                qT = aT2.A(hp * 2048 + gb * 128, [[1, ws[j]]], p0=64 * e_, np_=64)
                                k.op(PE, lambda kT=kT, qT=qT, e_=e_, j=j: nc.tensor.matmul(
                                    psum[sbk[e_]][:, j * 256:j * 256 + ws[j]], kT, qT, start=True, stop=True),
                                    reads=qreg + kreg, writes=[ps_reg[sbk[e_]]], inc=(j == 1))
                        for e_ in range(2):
                            h = g * 4 + 2 * hp + e_
                            sidx = 3 * e_ + pr % 3
                            treg = aT4.R(2048 + e_ * 512, 512)
                            preg = pt_regs(sidx)
                            if ws[0] == ws[1]:
                                segs = [(0, 2, ws[0])]
                            else:
                                segs = [(0, 1, ws[0]), (1, 1, ws[1])]
                            for (j0, nj, w) in segs:
                                tmp = aT4.A(2048 + e_ * 512 + j0 * 256, [[256, nj], [1, w]])
                                src = AP(psum[sbk[e_]][:].tensor, psum[sbk[e_]][:].offset + j0 * 256, [[512, P], [256, nj], [1, w]])
                                bsrc = bias_t[:, h:h + 1, 0:w].broadcast_to([P, nj, w])
                                k.op(DVE, lambda tmp=tmp, src=src, bsrc=bsrc: nc.vector.tensor_tensor(tmp, src, bsrc, ALU.add),
                                     reads=[ps_reg[sbk[e_]], r_const], writes=treg)
                                pt = pt_ap(sidx, j0 * 256, [[256, nj], [1, w]])
                                k.op(ACT, lambda pt=pt, tmp=tmp: nc.scalar.activation(pt, tmp, AF.Exp), reads=treg, writes=preg)

                    def att_out(pr, hp=hp):
                        gb2 = 2 * pr
                        ob = new_bank()
                        for e_ in range(2):
                            h = 2 * hp + e_
                            vc = (0, 64, 192, 256)[h]
                            scur = 3 * e_ + pr % 3
                            sprev = 3 * e_ + (pr - 1) % 3
                            for j in range(2):
                                gb = gb2 + j
                                n = gb % nb
                                col = e_ * 256 + j * 128
                                pairs = []
                                rds = []
                                if n > 0:
                                    if j == 0:
                                        pairs.append((aT3.A((gb - 1) * 384 + vc, [[1, 128]]), pt_ap(sprev, 256 + 128, [[1, 128]])))
                                        rds += pt_regs(sprev)
                                    else:
                                        pairs.append((aT3.A((gb - 1) * 384 + vc, [[1, 128]]), pt_ap(scur, 128, [[1, 128]])))
                                    rds += aT3.R((gb - 1) * 384, 384)
                                pairs.append((aT3.A(gb * 384 + vc, [[1, 128]]), pt_ap(scur, j * 256, [[1, 128]])))
                                rds += aT3.R(gb * 384, 384) + pt_regs(scur)
                                mm_group(ob, (col, col + 128), pairs, reads=rds, last_inc=(e_ == 1 and j == 1))
                        g0 = gb2
                        src = AP(psum[ob][:].tensor, psum[ob][:].offset, [[512, P], [256, 2], [1, 256]])
                        if g == 0:
                            dst = aT1.A(2 * hp * 2048 + 128 * g0, [[2048, 2], [1, 256]])
                        elif g == 1:
                            dst = aT1.A(2 * hp * 2048 + 512 * (g0 % 4) + g0 // 4, [[2048, 2], [4, 256]])
                        else:
                            src = AP(psum[ob][:].tensor, psum[ob][:].offset, [[512, P], [256, 2], [128, 2], [1, 128]])
                            dst = aT1.A(2 * hp * 2048 + g0, [[2048, 2], [1, 2], [16, 128]])
                        aregs = aT1.R(2 * hp * 2048, 4096)
                        if g == 0:
                            k.op(ACT, lambda dst=dst, src=src: nc.scalar.copy(dst, src), reads=[ps_reg[ob]], writes=aregs)
                        else:
                            k.op(DVE, lambda dst=dst, src=src: nc.vector.tensor_tensor(dst, src, dst, ALU.add),
                                 reads=[ps_reg[ob]] + aregs, writes=aregs)

                    for pr in range(9):
                        if pr < 8:
                            if hp == 0:
                                vproj(2 * pr)
                            att_scores(pr)
                        if pr >= 1:
                            att_out(pr - 1)

                chk('att%d' % g)
            for hp in range(2):
                j0, j1 = 2 * hp, 2 * hp + 1
                a0 = aT1.R(j0 * 2048, 2048)
                a1 = aT1.R(j1 * 2048, 2048)
                rr = aT4.R(0, 2048)
                for (jj, p0_, ar) in ((j0, 64, a0), (j1, 0, a1)):
                    dv = aT1.A(jj * 2048, [[1, S]], p0_, 64)
                    k.op(ACT, lambda dv=dv: nc.scalar.activation(dv, dv, AF.Ln), reads=ar, writes=ar)
                    k.op(ACT, lambda dv=dv: nc.scalar.activation(dv, dv, AF.Exp, scale=-1.0), reads=ar, writes=ar)
                k.op(DVE, lambda: nc.vector.tensor_copy(aT4.A(0, [[1, S]], 0, 64), aT1.A(j0 * 2048, [[1, S]], 64, 64)), reads=a0, writes=rr)
                k.op(DVE, lambda: nc.vector.tensor_copy(aT4.A(0, [[1, S]], 64, 64), aT1.A(j1 * 2048, [[1, S]], 0, 64)), reads=a1, writes=rr)
                k.op(DVE, lambda: nc.vector.tensor_tensor(attT[0:64, hp, :], aT1.A(j0 * 2048, [[1, S]], 0, 64), aT4.A(0, [[1, S]], 0, 64), ALU.mult),
                     reads=a0 + rr, writes=r_attT[hp])
                k.op(DVE, lambda: nc.vector.tensor_tensor(attT[64:128, hp, :], aT1.A(j1 * 2048, [[1, S]], 64, 64), aT4.A(0, [[1, S]], 64, 64), ALU.mult),
                     reads=a1 + rr, writes=r_attT[hp])
            if debug and b == 0:
                k.dma(SP, tl_dbg_for("attT"), [(dbg["attT"], attT[:])], reads=[r for rr_ in r_attT for r in rr_])

            chk('att')
            for hf in range(2):
                t0 = hf * HALF
                seq_start = (hf == 0)

                def hreg(kk, tt):
                    return r_hT[kk][2 * hf + tt]

                def p1_proj(pg_):
                    item_begin()
                    sp_ = wload(lambda s, pg_=pg_: [(wdst(s, KC, 0, 256), w_in_v[:, :, 2304 + 256 * pg_:2304 + 256 * pg_ + 256])])
                    ui = pg_ % 2
                    taus = list(range(1 if seq_start else 0, 9))
                    i = 0
                    while i < len(taus):
                        bk = new_bank()
                        grp = taus[i:i + 2]
                        for j, tau in enumerate(grp):
                            tk0 = t0 + 128 * (tau - 1)
                            tt_abs = tk0 // 512
                            mm_group(bk, (j * 256, j * 256 + 256),
                                     [(hT[:, kk, tk0:tk0 + 128], wv(sp_, kk, 0, 256)) for kk in range(KC)],
                                     reads=[r_w[sp_]] + [r_hT[kk][tt_abs] for kk in range(KC)], last_inc=(j == len(grp) - 1))
                        e = k.evac_eng()
                        dst = u_t[:, ui, grp[0]:grp[0] + len(grp), :]
                        src = psum[bk][:, 0:256 * len(grp)].rearrange("p (a c) -> p a c", c=256)
                        if e is ACT:
                            k.op(ACT, lambda dst=dst, src=src: nc.scalar.copy(dst, src), reads=[ps_reg[bk]], writes=[r_u[ui]])
                        else:
                            k.op(DVE, lambda dst=dst, src=src: nc.vector.tensor_copy(dst, src), reads=[ps_reg[bk]], writes=[r_u[ui]])
                        i += 2

                def p1_bands(pg_):
                    wpool = POOL_W[pg_]
                    ui = pg_ % 2
                    for cc in range(2):
                        pc = 2 * pg_ + cc
                        for tq in range(2):
                            bk = new_bank()
                            for s_ in range(4):
                                tau = 1 + 4 * tq + s_
                                first = seq_start and tau == 1
                                pairs = [(u_t[:, ui, tau, cc * 128:(cc + 1) * 128], bands_t[:, pg_ * 3 + (2 if first else 0), :])]
                                if not first:
                                    pairs.append((u_t[:, ui, tau - 1, cc * 128:(cc + 1) * 128], bands_t[:, pg_ * 3 + 1, :]))
                                mm_group(bk, (s_ * 128, s_ * 128 + 128), pairs, reads=[r_u[ui], r_bands], last_inc=(s_ == 3))
                            dst = aT2.A(pc * 1024 + tq * 512, [[1, 512]])
                            regs = aT2.R(pc * 1024 + tq * 512, 512)
                            e = k.evac_eng()
                            if e is ACT:
                                k.op(ACT, lambda dst=dst, bk=bk: nc.scalar.activation(dst, psum[bk][:, :], AF.Copy, scale=1.0 / wpool),
                                     reads=[ps_reg[bk]], writes=regs)
                            else:
                                k.op(DVE, lambda dst=dst, bk=bk: nc.vector.tensor_scalar(dst, psum[bk][:, :], 1.0 / wpool, None, ALU.mult),
                                     reads=[ps_reg[bk]], writes=regs)
                            if seq_start and tq == 0:
                                k.op(DVE, lambda bk=bk, pc=pc: nc.vector.tensor_tensor(
                                    aT2.A(pc * 1024, [[1, 16]]), psum[bk][:, 0:16], corr_t[:, pg_, :], ALU.mult),
                                    reads=[ps_reg[bk], r_const], writes=regs)
                for pg_ in range(4):
                    p1_proj(pg_)
                    if pg_ >= 1:
                        p1_bands(pg_ - 1)
                p1_bands(3)
                if debug and b == 0 and hf == 0:
                    k.dma(SP, tl_dbg_for("pmT"), [(dbg["pmT"], aT2.A(0, [[1024, KC], [1, HALF]]))], reads=aT2.R(0, 8192))

                chk('P1')
                item_begin()
                spg = wload(lambda s: [(wring[:, s, ci * 1024:(ci + 1) * 1024].rearrange("p (g o) -> p g o", g=4), w_pg_v[:, ci, :, :]) for ci in range(2)])
                for co in range(KC):
                    pg_, cc = co // 2, co % 2
                    for tt in range(2):
                        bk = new_bank()
                        pairs = [(wring[:, spg, ci * 1024 + pg_ * 256 + cc * 128: ci * 1024 + pg_ * 256 + cc * 128 + 128],
                                  aT2.A((2 * pg_ + ci) * 1024 + tt * 512, [[1, 512]])) for ci in range(2)]
                        rds = [r_w[spg]] + aT2.R((2 * pg_) * 1024 + tt * 512, 512) + aT2.R((2 * pg_ + 1) * 1024 + tt * 512, 512)
                        mm_group(bk, (0, 512), pairs, reads=rds)
                        dst = aT3.A(co * 1024 + tt * 512, [[1, 512]])
                        regs = aT3.R(co * 1024 + tt * 512, 512)
                        e = k.evac_eng()
                        if e is ACT:
                            k.op(ACT, lambda dst=dst, bk=bk, co=co: nc.scalar.activation(dst, psum[bk][:, :], AF.Copy, scale=prm[:, 4, co:co + 1]),
                                 reads=[ps_reg[bk], r_const], writes=regs)
                        else:
                            k.op(DVE, lambda dst=dst, bk=bk, co=co: nc.vector.tensor_scalar(dst, psum[bk][:, :], prm[:, 4, co:co + 1], None, ALU.mult),
                                 reads=[ps_reg[bk], r_const], writes=regs)
                if debug and b == 0 and hf == 0:
                    k.dma(SP, tl_dbg_for("pgT"), [(dbg["pgT"], aT3.A(0, [[1024, KC], [1, HALF]]))], reads=aT3.R(0, 8192))

                chk('P2')
                for m in range(KC):
                    yreg_all = aT1.R(m * 1024, 1024)
                    k.dma(SP, tl_y[m], [(aT1.A(m * 1024, [[1, HALF]]), xT[b, m * P:(m + 1) * P, t0:t0 + HALF])], writes=yreg_all)
                    k.op(ACT, lambda m=m: nc.scalar.activation(
                        aT1.A(m * 1024, [[1, HALF]]), aT1.A(m * 1024, [[1, HALF]]), AF.Copy, scale=ALPHA),
                        reads=yreg_all, writes=yreg_all)
                for m2 in range(4):
                    item_begin()
                    sga = wload(lambda s, m2=m2: [(wdst(s, KC, 0, 256), w_in_v[:, :, 3328 + 256 * m2:3328 + 256 * m2 + 256])])
                    sgb = wload(lambda s, m2=m2: [(wdst(s, KC, 0, 256), w_in_v[:, :, 4352 + 256 * m2:4352 + 256 * m2 + 256])])
                    sbp = wload(lambda s, m2=m2: [(wdst(s, KC, 0, 256), w_bp_v[:, :, 256 * m2:256 * m2 + 256])])
                    for mi in range(2):
                        m = 2 * m2 + mi
                        for tt in range(2):
                            tk = slice(t0 + tt * 512, t0 + tt * 512 + 512)
                            hrd = [hreg(kk, tt) for kk in range(KC)]
                            b_ga = new_bank()
                            mm_group(b_ga, (0, 512), [(wv(sga, kk, mi * 128, 128), hT[:, kk, tk]) for kk in range(KC)], reads=[r_w[sga]] + hrd)
                            b_gb = new_bank()
                            mm_group(b_gb, (0, 512), [(wv(sgb, kk, mi * 128, 128), hT[:, kk, tk]) for kk in range(KC)], reads=[r_w[sgb]] + hrd)
                            b_a = new_bank()
                            mm_group(b_a, (0, 512), [(watt[:, kc, m * 128:(m + 1) * 128], attT[:, kc, tk]) for kc in range(2)],
                                     reads=[r_watt] + [r_attT[kc][2 * hf + tt] for kc in range(2)])
                            b_b = new_bank()
                            mm_group(b_b, (0, 512), [(wv(sbp, kk, mi * 128, 128), aT3.A(kk * 1024 + tt * 512, [[1, 512]])) for kk in range(KC)],
                                     reads=[r_w[sbp]] + [r_ for kk in range(KC) for r_ in aT3.R(kk * 1024 + tt * 512, 512)])
                            si = 2 * (tt % 2)
                            k.op(ACT, lambda b_ga=b_ga, si=si: nc.scalar.activation(sg_t[:, si, :], psum[b_ga][:, :], AF.Sigmoid),
                                 reads=[ps_reg[b_ga]], writes=[r_sg[si]])
                            k.op(ACT, lambda b_gb=b_gb, si=si: nc.scalar.activation(sg_t[:, si + 1, :], psum[b_gb][:, :], AF.Sigmoid),
                                 reads=[ps_reg[b_gb]], writes=[r_sg[si + 1]])
                            k.op(DVE, lambda b_a=b_a, si=si: nc.vector.tensor_tensor(t12_t[:, 0, :], psum[b_a][:, :], sg_t[:, si, :], ALU.mult),
                                 reads=[ps_reg[b_a], r_sg[si]], writes=[r_t12[0]])
                            k.op(DVE, lambda b_b=b_b, si=si: nc.vector.tensor_tensor(t12_t[:, 1, :], psum[b_b][:, :], sg_t[:, si + 1, :], ALU.mult),
                                 reads=[ps_reg[b_b], r_sg[si + 1]], writes=[r_t12[1]])
                            mreg = aT2.R(m * 1024 + tt * 512, 512)
                            k.op(DVE, lambda m=m, tt=tt: nc.vector.tensor_tensor(
                                aT2.A(m * 1024 + tt * 512, [[1, 512]]), t12_t[:, 0, :], t12_t[:, 1, :], ALU.add),
                                reads=r_t12, writes=mreg)
                if debug and b == 0 and hf == 0:
                    k.dma(SP, tl_dbg_for("mrg"), [(dbg["mrg"], aT2.A(0, [[1024, KC], [1, HALF]]))], reads=aT2.R(0, 8192))

                chk('D1')
                stat = [[new_bank(hold=True) for _ in range(2)] for _ in range(2)]

                feed = {"n": 0, "q": []}

                def ln_stats_feed(m, tt, last):
                    yreg = aT1.R(m * 1024 + tt * 512, 512)
                    ysrc = aT1.A(m * 1024 + tt * 512, [[1, 512]])
                    i0 = 2 * (feed["n"] % 3)
                    feed["n"] += 1
                    k.op(ACT, lambda: nc.scalar.copy(ybs_t[:, i0, :], ysrc), reads=yreg, writes=[r_ybs[i0]])
                    k.op(ACT, lambda: nc.scalar.activation(ybs_t[:, i0 + 1, :], ysrc, AF.Square), reads=yreg, writes=[r_ybs[i0 + 1]])
                    feed["q"].append((m, tt, last, i0))
                    while len(feed["q"]) > 2:
                        ln_stats_pe(*feed["q"].pop(0))

                def ln_stats_flush():
                    while feed["q"]:
                        ln_stats_pe(*feed["q"].pop(0))

                def ln_stats_pe(m, tt, last, i0):
                    for j in range(2):
                        bk = stat[tt][j]
                        k.op(PE, lambda bk=bk, j=j: nc.tensor.matmul(psum[bk][:, :], onesm[:], ybs_t[:, i0 + j, :], start=(m == 0), stop=last),
                             reads=[r_ybs[i0 + j], r_const], writes=[ps_reg[bk]], inc=True)

                def ln_finish(tt):
                    bm, bq = stat[tt]
                    k.op(ACT, lambda: nc.scalar.activation(st_t[:, 0, :], psum[bm][:, :], AF.Square), reads=[ps_reg[bm]], writes=[r_st[0]])
                    k.op(DVE, lambda: nc.vector.tensor_tensor(st_t[:, 1, :], psum[bq][:, :], st_t[:, 0, :], ALU.subtract),
                         reads=[ps_reg[bq], r_st[0]], writes=[r_st[1]])
                    k.op(ACT, lambda: nc.scalar.activation(st_t[:, 1, :], st_t[:, 1, :], AF.Ln, bias=eps_t[:, 0:1]),
                         reads=[r_st[1], r_const], writes=[r_st[1]])
                    k.op(ACT, lambda: nc.scalar.activation(st_t[:, 2, :], st_t[:, 1, :], AF.Exp, scale=-0.5), reads=[r_st[1]], writes=[r_st[2]])
                    k.op(DVE, lambda: nc.vector.scalar_tensor_tensor(st_t[:, 3, :], psum[bm][:, :], -1.0, st_t[:, 2, :], ALU.mult, ALU.mult),
                         reads=[ps_reg[bm], r_st[2]], writes=[r_st[3]])
                    ps_held[bm] = False
                    ps_held[bq] = False

                def ln_norm(m, tt, ti):
                    yreg = aT1.R(m * 1024 + tt * 512, 512)
                    ysrc = aT1.A(m * 1024 + tt * 512, [[1, 512]])
                    k.op(DVE, lambda: nc.vector.tensor_tensor(t12_t[:, ti, :], ysrc, st_t[:, 2, :], ALU.mult),
                         reads=yreg + [r_st[2]], writes=[r_t12[ti]])
                    k.op(DVE, lambda: nc.vector.tensor_tensor(t12_t[:, ti, :], t12_t[:, ti, :], st_t[:, 3, :], ALU.add),
                         reads=[r_t12[ti], r_st[3]], writes=[r_t12[ti]])

                def ln1_apply_chunk(tt, m):
                    ti = m % 2
                    ln_norm(m, tt, ti)
                    yreg = aT1.R(m * 1024 + tt * 512, 512)
                    ydst = aT1.A(m * 1024 + tt * 512, [[1, 512]])
                    hreg2 = aT3.R(m * 1024 + tt * 512, 512)
                    k.op(ACT, lambda: nc.scalar.activation(
                        ydst, t12_t[:, ti, :], AF.Identity, bias=aGB[:, 1, m:m + 1], scale=aGB[:, 0, m:m + 1]),
                        reads=[r_t12[ti], r_m1], writes=yreg)
                    k.op(ACT, lambda: nc.scalar.activation(
                        aT3.A(m * 1024 + tt * 512, [[1, 512]]), t12_t[:, ti, :], AF.Identity, bias=sc(B2, m, b), scale=sc(G2, m, b)),
                        reads=[r_t12[ti], r_m2], writes=hreg2)

                item_begin()
                sos = [wload(lambda s, m2=m2: [(wdst(s, KC, 0, 256), w_out_v[:, :, 256 * m2:256 * m2 + 256])]) for m2 in range(4)]
                for tt in range(2):
                    for m in range(KC):
                        so, mi = sos[m // 2], m % 2
                        bk = new_bank()
                        mm_group(bk, (0, 512), [(wv(so, kk, mi * 128, 128), aT2.A(kk * 1024 + tt * 512, [[1, 512]])) for kk in range(KC)],
                                 reads=[r_w[so]] + [r_ for kk in range(KC) for r_ in aT2.R(kk * 1024 + tt * 512, 512)])
                        yreg = aT1.R(m * 1024 + tt * 512, 512)
                        ydst = aT1.A(m * 1024 + tt * 512, [[1, 512]])
                        k.op(DVE, lambda bk=bk, ydst=ydst, m=m: nc.vector.scalar_tensor_tensor(
                            ydst, psum[bk][:, :], sc(modT, 16 + m, b), ydst, ALU.mult, ALU.add),
                            reads=[ps_reg[bk], r_m2] + yreg, writes=yreg)
                        ln_stats_feed(m, tt, last=(m == KC - 1))
                        if tt == 1:
                            if m == 1:
                                ln_finish(0)
                            if m >= 2:
                                ln1_apply_chunk(0, m - 2)
                ln1_apply_chunk(0, KC - 2)
                ln1_apply_chunk(0, KC - 1)
                ln_stats_flush()
                ln_finish(1)
                for m in range(KC):
                    ln1_apply_chunk(1, m)
                if debug and b == 0 and hf == 0:
                    k.dma(SP, tl_dbg_for("y1h2"), [(dbg["y1"], aT1.A(0, [[1024, KC], [1, HALF]])), (dbg["h2"], aT3.A(0, [[1024, KC], [1, HALF]]))],
                          reads=aT1.R(0, 8192) + aT3.R(0, 8192))

                if hf == 1 and b + 1 < NB:
                    emit_S0(b + 1, early=True)
                chk('D2')
                for fgi, (f0, f1) in enumerate(FGROUPS):
                    nf = f1 - f0
                    fl = 0
                    while fl < nf:
                        nch = min(4 if (fgi == 0 and fl == 0) else 2, nf - fl)
                        item_begin()
                        sl_g, sl_u = [], []
                        for c_ in range(0, nch, 2):
                            ncols = min(2, nch - c_) * 128
                            c0 = (f0 + fl + c_) * 128
                            sl_g.append(wload(lambda s, c0=c0, ncols=ncols: [(wdst(s, KC, 0, ncols), w_gate_v[:, :, c0:c0 + ncols])]))
                            sl_u.append(wload(lambda s, c0=c0, ncols=ncols: [(wdst(s, KC, 0, ncols), w_up_v[:, :, c0:c0 + ncols])]))
                        for tt in range(2):
                            for fi in range(nch):
                                sgt, sup, fo = sl_g[fi // 2], sl_u[fi // 2], (fi % 2) * 128
                                h2r = [r_ for kk in range(KC) for r_ in aT3.R(kk * 1024 + tt * 512, 512)]
                                bg = new_bank()
                                mm_group(bg, (0, 512), [(wv(sgt, kk, fo, 128), aT3.A(kk * 1024 + tt * 512, [[1, 512]])) for kk in range(KC)],
                                         reads=[r_w[sgt]] + h2r)
                                bu = new_bank()
                                mm_group(bu, (0, 512), [(wv(sup, kk, fo, 128), aT3.A(kk * 1024 + tt * 512, [[1, 512]])) for kk in range(KC)],
                                         reads=[r_w[sup]] + h2r)
                                si = (2 * fi + tt) % 4
                                k.op(ACT, lambda bg=bg, si=si: nc.scalar.activation(sg_t[:, si, :], psum[bg][:, :], AF.Silu),
                                     reads=[ps_reg[bg]], writes=[r_sg[si]])
                                areg = aT4.R(((fl + fi) * 1024 + tt * 512) // 2, 256)
                                k.op(DVE, lambda bu=bu, si=si, fl=fl, fi=fi, tt=tt: nc.vector.tensor_tensor(
                                    actT_ap(fl + fi, tt * 512, 512), psum[bu][:, :], sg_t[:, si, :], ALU.mult),
                                    reads=[ps_reg[bu], r_sg[si]], writes=areg)
                        fl += nch
                    last_group = (fgi == len(FGROUPS) - 1)

                    def dn_step(sd, mi, m, tt, nf=nf):
                        bk = new_bank()
                        mm_group(bk, (0, 512), [(wv(sd, kf, mi * 128, 128), actT_ap(kf, tt * 512, 512)) for kf in range(nf)],
                                 reads=[r_w[sd]] + [r_ for kf in range(nf) for r_ in aT4.R((kf * 1024 + tt * 512) // 2, 256)])
                        yreg = aT1.R(m * 1024 + tt * 512, 512)
                        ydst = aT1.A(m * 1024 + tt * 512, [[1, 512]])
                        k.op(DVE, lambda: nc.vector.scalar_tensor_tensor(
                            ydst, psum[bk][:, :], sc(modT, 40 + m, b), ydst, ALU.mult, ALU.add),
                            reads=[ps_reg[bk], r_m2] + yreg, writes=yreg)

                    if not last_group:
                        for m2 in range(4):
                            item_begin()
                            sd = wload(lambda s, nf=nf, f0=f0, f1=f1, m2=m2: [(wdst(s, nf, 0, 256), w_down_v[:, f0:f1, 256 * m2:256 * m2 + 256])])
                            for mi in range(2):
                                for tt in range(2):
                                    dn_step(sd, mi, 2 * m2 + mi, tt)
                    else:
                        stat = [[new_bank(hold=True) for _ in range(2)] for _ in range(2)]
                        oi_ = [0]

                        def ln2_apply_chunk(tt, m):
                            ti = m % 2
                            ln_norm(m, tt, ti)
                            oi_[0] = (oi_[0] + 1) % 2
                            oi = oi_[0]
                            k.op(ACT, lambda: nc.scalar.activation(
                                ost_t[:, oi, :], t12_t[:, ti, :], AF.Identity, bias=prm[:, 3, m:m + 1], scale=prm[:, 2, m:m + 1]),
                                reads=[r_t12[ti], r_const], writes=[r_ost[oi]])
                            k.dma(SP, tl_o[oi], [(outT[b, m * P:(m + 1) * P, t0 + tt * 512:t0 + tt * 512 + 512], ost_t[:, oi, :])],
                                  reads=[r_ost[oi]])

                        item_begin()
                        sds = [wload(lambda s, nf=nf, f0=f0, f1=f1, m2=m2: [(wdst(s, nf, 0, 256), w_down_v[:, f0:f1, 256 * m2:256 * m2 + 256])])
                               for m2 in range(4)]
                        for tt in range(2):
                            for m in range(KC):
                                dn_step(sds[m // 2], m % 2, m, tt)
                                ln_stats_feed(m, tt, last=(m == KC - 1))
                                if tt == 1:
                                    if m == 1:
                                        ln_finish(0)
                                    if m >= 2:
                                        ln2_apply_chunk(0, m - 2)
                        ln2_apply_chunk(0, KC - 2)
                        ln2_apply_chunk(0, KC - 1)
                        chk('FFN')
                        ln_stats_flush()
                        ln_finish(1)
                        for m in range(KC):
                            ln2_apply_chunk(1, m)

        def emit_all():
            try:
                emit_consts()
                for it_ in range(8):
                    emit_mod(it_)
                emit_derived1()
                chk('mod')
                for b_ in range(NB):
                    seq_body(b_)
            except _Stop:
                pass

        k.dry = True
        emit_all()
        k.dry = False
        ps_next[0] = 0
        for i_ in range(8):
            ps_held[i_] = False
        k.flip = 0
        s0_done.clear()
        emit_all()

        for tl in tl_o + list(tl_dbgs.values()):
            if tl.cnt:
                nc.sync.wait_ge(tl.sem, tl.cnt)
    return nc


_NC_CACHE = {}


def _get_nc(debug=False):
    if debug not in _NC_CACHE:
        _NC_CACHE[debug] = build_program(debug)
    return _NC_CACHE[debug]


def make_in_maps(inputs):
    f = lambda a: np.ascontiguousarray(np.asarray(a, dtype=np.float32))
    x = f(inputs["x"])
    c = f(inputs["c"])
    vecT = lambda v, n: f(np.asarray(v, np.float32).reshape(n, P).T)
    bias, bands, corr = const_tables()
    shared = {
        "w_ada": f(inputs["w_ada"][0]), "b_adaT": vecT(inputs["b_ada"][0], 48),
        "w_in": f(inputs["w_in"][0]), "w_batt": f(inputs["w_branch_att"][0]),
        "w_pg": f(inputs["w_pool_group"][0]), "pscaleT": vecT(inputs["pool_scale"][0], KC),
        "w_bp": f(inputs["w_branch_pool"][0]), "w_out": f(inputs["w_out"][0]),
        "ln1gT": vecT(inputs["ln1_g"][0], KC), "ln1bT": vecT(inputs["ln1_b"][0], KC),
        "ln2gT": vecT(inputs["ln2_g"][0], KC), "ln2bT": vecT(inputs["ln2_b"][0], KC),
        "w_gate": f(inputs["w_gate"][0]), "w_up": f(inputs["w_up"][0]), "w_down": f(inputs["w_down"][0]),
        "bias_tab": bias, "bands": bands, "corr": corr,
    }
    in_maps = []
    for i in range(N_CORES):
        xs = x[NB * i:NB * (i + 1)]
        m = dict(shared)
        m["xT"] = f(xs.transpose(0, 2, 1))
        cs = c[NB * i:NB * (i + 1)]
        m["cT"] = f(cs.reshape(NB, KC, P).transpose(2, 1, 0))
        in_maps.append(m)
    return in_maps


def kernel(**inputs):
    nc = _get_nc(False)
    in_maps = make_in_maps(inputs)
    res = run_bass_kernel_spmd(nc, in_maps, core_ids=list(range(N_CORES)))
    out = np.empty((NB * N_CORES, S, D), np.float32)
    for i in range(N_CORES):
        out[NB * i:NB * (i + 1)] = np.asarray(res.results[i]["outT"]).transpose(0, 2, 1)
    return out
```

```python
import contextlib
import math

import numpy as np

import concourse.bass as bass
import concourse.mybir as mybir
from concourse.ap import AP
from concourse.bass_utils import run_bass_kernel_spmd

F32 = mybir.dt.float32
BF16 = mybir.dt.bfloat16
ALU = mybir.AluOpType
AF = mybir.ActivationFunctionType

P = 128
D = 1024
KC = 8
S = 2048
HALF = 1024
NB = 2
N_CORES = 8
IN_W = 5376
DFF = 2816
NFC = 22
ALPHA = 2.0 ** 0.25
LN_EPS = 1e-5
DIL = (1, 4, 16)
NBLK = (16, 4, 1)
POOL_W = (2, 4, 8, 16)
NW = 6
FGROUPS = ((0, 8), (8, 15), (15, 22))


def alibi_slopes_np(n):
    def pow2(m):
        start = 2.0 ** (-8.0 / m)
        return [start ** (i + 1) for i in range(m)]
    if math.log2(n).is_integer():
        s = pow2(n)
    else:
        c = 2 ** math.floor(math.log2(n))
        s = pow2(c) + pow2(2 * c)[0::2][: n - c]
    return np.array(sorted(s, reverse=True), dtype=np.float32)


def const_tables():
    slopes = alibi_slopes_np(12).reshape(3, 4)
    k = np.arange(128)[:, None]
    c = np.arange(256)[None, :]
    diff = np.where(c < 128, c - k, 128 + (c - 128) - k)
    valid = (diff >= 0) & (diff <= 128)
    bias = np.empty((128, 12, 256), np.float32)
    for g in range(3):
        for h in range(4):
            b = -(slopes[g, h] * (diff * DIL[g]).astype(np.float32)).astype(np.float32)
            bias[:, g * 4 + h, :] = np.where(valid, b, np.float32(-30000.0))
    tp = np.arange(128)[:, None]
    t = np.arange(128)[None, :]
    bands = np.zeros((128, 12, 128), np.float32)
    corr = np.zeros((128, 4, 16), np.float32)
    for g, w in enumerate(POOL_W):
        inwin = ((t - tp) >= 0) & ((t - tp) < w)
        eye = (t == tp)
        bands[:, g * 3 + 0, :] = inwin.astype(np.float32) - w * eye
        prev = ((t + 128 - tp) >= 0) & ((t + 128 - tp) < w)
        bands[:, g * 3 + 1, :] = prev.astype(np.float32)
        cnt = np.minimum(t + 1, w).astype(np.float32)
        bands[:, g * 3 + 2, :] = inwin.astype(np.float32) - cnt * eye
        corr[:, g, :] = (1.0 / np.minimum(np.arange(16) + 1, w)).astype(np.float32)[None, :]
    return bias, bands, corr


class Tl:
    def __init__(self, sem):
        self.sem = sem
        self.cnt = 0


class Reg:
    __slots__ = ("w", "r")

    def __init__(self):
        self.w = None
        self.r = {}


class Eng:
    def __init__(self, eng, sem, is_pe=False):
        self.eng = eng
        self.tl = Tl(sem)
        self.seen = {}
        self.is_pe = is_pe


class Arena:
    def __init__(self, t, n_elems, gran):
        self.ap = t[:]
        self.tensor = self.ap.tensor
        self.pitch = self.ap.ap[0][0]
        self.gran = gran
        self.regs = [Reg() for _ in range((n_elems + gran - 1) // gran)]

    def R(self, off, n):
        return self.regs[off // self.gran:(off + n - 1) // self.gran + 1]

    def A(self, off, dims, p0=0, np_=P):
        return AP(self.tensor, p0 * self.pitch + off, [[self.pitch, np_]] + [list(d) for d in dims])


class K:
    def __init__(self, nc, es):
        self.nc = nc
        self.es = es
        self.nsem = 0
        self.pe = Eng(nc.tensor, self.sem("tl_pe"), True)
        self.act = Eng(nc.scalar, self.sem("tl_act"))
        self.dve = Eng(nc.vector, self.sem("tl_dve"))
        self.pool = Eng(nc.gpsimd, self.sem("tl_pool"))
        self.sp = Eng(nc.sync, self.sem("tl_sp"))
        self.flip = 0
        self.dry = False

    def sem(self, name):
        self.nsem += 1
        return self.es.enter_context(self.nc.semaphore(name))

    def sb(self, name, shape, dt):
        return self.es.enter_context(self.nc.sbuf_tensor(name, shape, dt))

    def _waits(self, e, reads, writes):
        deps = {}

        def add(tok, raw):
            if tok is None:
                return
            tl, v = tok
            if tl is e.tl and e.is_pe:
                return
            if deps.get(tl, 0) < v:
                deps[tl] = v
        for r in reads:
            add(r.w, True)
        for w in writes:
            add(w.w, False)
            for tok in w.r.values():
                add(tok, False)
        for tl, v in deps.items():
            if e.seen.get(tl, 0) < v:
                e.eng.wait_ge(tl.sem, v)
                e.seen[tl] = v

    def op(self, e, fn, reads=(), writes=(), inc=True):
        if self.dry:
            return None
        self._waits(e, reads, writes)
        tick = e.tl.cnt + 1
        ins = fn()
        if inc:
            ins.then_inc(e.tl.sem, 1)
            e.tl.cnt = tick
        tok = (e.tl, tick)
        for r in reads:
            r.r[e.tl] = tok
        for w in writes:
            w.w = tok
            w.r = {}
        return ins

    def dma(self, q, tl, xfers, reads=(), writes=()):
        if self.dry:
            return
        self._waits(q, reads, writes)
        if tl.cnt and q.seen.get(tl, 0) < tl.cnt:
            q.eng.wait_ge(tl.sem, tl.cnt)
            q.seen[tl] = tl.cnt
        for (o, i) in xfers:
            q.eng.dma_start(out=o, in_=i).then_inc(tl.sem, 16)
            tl.cnt += 16
        tok = (tl, tl.cnt)
        for r in reads:
            r.r[tl] = tok
        for w in writes:
            w.w = tok
            w.r = {}

    def evac_eng(self):
        self.flip ^= 1
        return self.act if self.flip else self.dve


class _Stop(Exception):
    pass


def build_program(debug=False, stop_after=None):
    nc = bass.Bass("TRN2", target_bir_lowering=False)

    def chk(name):
        if stop_after == name:
            raise _Stop()

    def din(name, shape):
        return nc.dram_tensor(name, shape, F32, kind="ExternalInput").ap()
    xT = din("xT", [NB, D, S])
    cT = din("cT", [P, KC, NB])
    w_ada = din("w_ada", [D, 6 * D])
    b_adaT = din("b_adaT", [P, 48])
    w_in = din("w_in", [D, IN_W])
    w_batt = din("w_batt", [256, D])
    w_pg = din("w_pg", [4, 256, 256])
    pscaleT = din("pscaleT", [P, KC])
    w_bp = din("w_bp", [D, D])
    w_out = din("w_out", [D, D])
    ln1gT = din("ln1gT", [P, KC])
    ln1bT = din("ln1bT", [P, KC])
    ln2gT = din("ln2gT", [P, KC])
    ln2bT = din("ln2bT", [P, KC])
    w_gate = din("w_gate", [D, DFF])
    w_up = din("w_up", [D, DFF])
    w_down = din("w_down", [DFF, D])
    bias_d = din("bias_tab", [P, 12, 256])
    bands_d = din("bands", [P, 12, 128])
    corr_d = din("corr", [P, 4, 16])
    outT = nc.dram_tensor("outT", [NB, D, S], F32, kind="ExternalOutput").ap()
    dbg = {}
    if debug:
        dbg["modT"] = nc.dram_tensor("dbg_modT", [P, 48, NB], F32, kind="ExternalOutput").ap()
        dbg["hT"] = nc.dram_tensor("dbg_hT", [P, KC, S], BF16, kind="ExternalOutput").ap()
        dbg["attT"] = nc.dram_tensor("dbg_attT", [P, 2, S], BF16, kind="ExternalOutput").ap()
        dbg["pmT"] = nc.dram_tensor("dbg_pmT", [P, KC, HALF], BF16, kind="ExternalOutput").ap()
        dbg["pgT"] = nc.dram_tensor("dbg_pgT", [P, KC, HALF], BF16, kind="ExternalOutput").ap()
        dbg["mrg"] = nc.dram_tensor("dbg_mrg", [P, KC, HALF], BF16, kind="ExternalOutput").ap()
        dbg["y1"] = nc.dram_tensor("dbg_y1", [P, KC, HALF], F32, kind="ExternalOutput").ap()
        dbg["h2"] = nc.dram_tensor("dbg_h2", [P, KC, HALF], BF16, kind="ExternalOutput").ap()

    w_in_v = w_in.rearrange("(k p) n -> p k n", p=P)
    w_ada_v = w_ada.rearrange("(k p) n -> p k n", p=P)
    w_bp_v = w_bp.rearrange("(k p) n -> p k n", p=P)
    w_out_v = w_out.rearrange("(k p) n -> p k n", p=P)
    w_gate_v = w_gate.rearrange("(k p) n -> p k n", p=P)
    w_up_v = w_up.rearrange("(k p) n -> p k n", p=P)
    w_down_v = w_down.rearrange("(k p) n -> p k n", p=P)
    w_batt_v = w_batt.rearrange("(k p) n -> p k n", p=P)
    w_pg_v = w_pg.rearrange("g (c p) o -> p c g o", p=P)

    with contextlib.ExitStack() as es:
        k = K(nc, es)
        PE, ACT, DVE, POOL, SP = k.pe, k.act, k.dve, k.pool, k.sp

        bias_t = k.sb("bias_t", [P, 12, 256], F32)
        bands_t = k.sb("bands_t", [P, 12, 128], BF16)
        corr_t = k.sb("corr_t", [P, 4, 16], F32)
        onesm = k.sb("onesm", [P, P], BF16)
        cT_t = k.sb("cT_t", [P, KC, NB], F32)
        scT_t = k.sb("scT_t", [P, KC, NB], BF16)
        badaT_t = k.sb("badaT_t", [P, 48], F32)
        modT = k.sb("modT", [P, 48, NB], F32)
        prm = k.sb("prm", [P, 5, KC], F32)
        A1 = k.sb("A1", [P, KC, NB], F32)
        A2 = k.sb("A2", [P, KC, NB], F32)
        G2 = k.sb("G2", [P, KC, NB], F32)
        B2 = k.sb("B2", [P, KC, NB], F32)
        aGB = k.sb("aGB", [P, 2, KC], F32)
        watt = k.sb("watt", [P, 2, D], BF16)
        hT = k.sb("hT", [P, KC, S], BF16)
        attT = k.sb("attT", [P, 2, S], BF16)
        wring = k.sb("wring", [P, NW, KC * 256], BF16)
        T1 = k.sb("T1", [P, 8192], F32)
        T2 = k.sb("T2", [P, 8192], BF16)
        T3 = k.sb("T3", [P, 8192], BF16)
        T4 = k.sb("T4", [P, 4096], F32)
        u_t = k.sb("u_t", [P, 2, 9, 256], BF16)
        sg_t = k.sb("sg_t", [P, 4, 512], F32)
        t12_t = k.sb("t12_t", [P, 2, 512], F32)
        ybs_t = k.sb("ybs_t", [P, 6, 512], BF16)
        st_t = k.sb("st_t", [P, 4, 512], F32)
        ost_t = k.sb("ost_t", [P, 2, 512], F32)
        eps_t = k.sb("eps_t", [P, 1], F32)

        psum = [es.enter_context(nc.psum_tensor(f"ps{i}", [P, 512], F32)) for i in range(8)]
        ps_reg = [Reg() for _ in range(8)]
        ps_held = [False] * 8
        ps_next = [0]

        def new_bank(hold=False):
            for _ in range(16):
                i = ps_next[0]
                ps_next[0] = (i + 1) % 8
                if not ps_held[i]:
                    ps_held[i] = hold
                    return i
            raise RuntimeError("no psum bank")

        aT1 = Arena(T1, 8192, 512)
        aT2 = Arena(T2, 8192, 512)
        aT3 = Arena(T3, 8192, 512)
        aT4 = Arena(T4, 4096, 256)
        T4b = T4[:].bitcast(BF16)
        T4b_pitch = T4b.ap[0][0]

        def actT_ap(fl, off, n):
            return AP(T4b.tensor, T4b.offset + fl * 1024 + off, [[T4b_pitch, P], [1, n]])

        r_hT = [[Reg() for _ in range(4)] for _ in range(KC)]
        r_attT = [[Reg() for _ in range(4)] for _ in range(2)]
        r_w = [Reg() for _ in range(NW)]
        r_const = Reg()
        r_m1 = Reg()
        r_m2 = Reg()
        r_u = [Reg(), Reg()]
        r_sg = [Reg() for _ in range(4)]
        r_t12 = [Reg(), Reg()]
        r_ybs = [Reg() for _ in range(6)]
        r_st = [Reg() for _ in range(4)]
        r_ost = [Reg() for _ in range(2)]
        r_watt = Reg()
        r_bands = Reg()

        tl_w = [Tl(k.sem(f"dw{i}")) for i in range(NW)]
        tl_c = Tl(k.sem("dconst"))
        tl_c2 = Tl(k.sem("dconst2"))
        tl_x = [Tl(k.sem(f"dx{i}")) for i in range(4)]
        tl_o = [Tl(k.sem(f"do{i}")) for i in range(2)]
        tl_dbgs = {}

        def tl_dbg_for(name):
            if k.dry:
                return None
            tl_dbgs[name] = Tl(k.sem("ddbg_" + name))
            return tl_dbgs[name]
        tl_y = [Tl(k.sem(f"dy{i}")) for i in range(8)]

        w_next = [0]

        def wslot():
            i = w_next[0]
            w_next[0] = (i + 1) % NW
            return i

        def wv(slot, kk, c0, n):
            return wring[:, slot, kk * 256 + c0: kk * 256 + c0 + n]

        plan = []
        wst = {"req": 0, "emitted": 0}

        def item_begin():
            if k.dry:
                return
            hi = min(len(plan), wst["req"] + NW)
            while wst["emitted"] < hi:
                j = wst["emitted"]
                s = j % NW
                k.dma(POOL, tl_w[s], plan[j](s), writes=[r_w[s]])
                wst["emitted"] += 1

        def wload(xfers_fn):
            if k.dry:
                plan.append(xfers_fn)
                return (len(plan) - 1) % NW
            j = wst["req"]
            assert j < wst["emitted"], "wload without item_begin"
            wst["req"] += 1
            return j % NW

        def wdst(slot, nk, c0, n):
            return wring[:, slot, :].rearrange("p (k c) -> p k c", c=256)[:, 0:nk, c0:c0 + n]

        def emit_consts():
            k.dma(SP, tl_c, [
                (bias_t[:], bias_d), (corr_t[:], corr_d), (cT_t[:], cT), (badaT_t[:], b_adaT),
                (prm[:, 0, :], ln1gT), (prm[:, 1, :], ln1bT), (prm[:, 2, :], ln2gT), (prm[:, 3, :], ln2bT),
                (prm[:, 4, :], pscaleT),
            ], writes=[r_const])
            k.dma(POOL, tl_c2, [(bands_t[:], bands_d), (watt[:], w_batt_v)], writes=[r_bands, r_watt])
            k.op(DVE, lambda: nc.vector.memset(onesm[:], 1.0 / D), writes=[r_const])
            k.op(DVE, lambda: nc.vector.memset(eps_t[:], LN_EPS), writes=[r_const])
            k.op(ACT, lambda: nc.scalar.activation(scT_t[:], cT_t[:], AF.Silu), reads=[r_const], writes=[r_m1])
            k.op(DVE, lambda: nc.vector.tensor_scalar(aGB[:], prm[:, 0:2, :], ALPHA, None, ALU.mult), reads=[r_const], writes=[r_m1])

        def emit_mod(it):
            rm = r_m1 if it < 8 else r_m2
            item_begin()
            sl = wload(lambda s: [(wdst(s, KC, 0, 256), w_ada_v[:, :, 256 * it:256 * it + 256])])
            bk = new_bank()
            for mc in range(2):
                for kk in range(KC):
                    k.op(PE, lambda kk=kk, mc=mc: nc.tensor.matmul(
                        psum[bk][:, mc * 2:mc * 2 + 2], wv(sl, kk, mc * 128, 128), scT_t[:, kk, :], start=(kk == 0), stop=(kk == KC - 1)),
                        reads=[r_w[sl], r_m1], writes=[ps_reg[bk]], inc=(mc == 1 and kk == KC - 1))
            j0 = 2 * it
            k.op(DVE, lambda: nc.vector.tensor_tensor(
                modT[:, j0:j0 + 2, :], psum[bk][:, 0:4].rearrange("p (j b) -> p j b", b=NB),
                badaT_t[:, j0:j0 + 2].unsqueeze(2).broadcast_to([P, 2, NB]), ALU.add),
                reads=[ps_reg[bk], r_const], writes=[rm])

        def emit_derived1():
            k.op(DVE, lambda: nc.vector.tensor_scalar(A1[:], modT[:, 8:16, :], 1.0, None, ALU.add), reads=[r_m1], writes=[r_m1])

        def emit_derived2():
            k.op(DVE, lambda: nc.vector.tensor_scalar(A2[:], modT[:, 32:40, :], 1.0, None, ALU.add), reads=[r_m2], writes=[r_m2])
            k.op(DVE, lambda: nc.vector.tensor_tensor(
                G2[:], A2[:], prm[:, 0, :].unsqueeze(2).broadcast_to([P, KC, NB]), ALU.mult), reads=[r_m2, r_const], writes=[r_m2])
            k.op(DVE, lambda: nc.vector.tensor_tensor(
                B2[:], A2[:], prm[:, 1, :].unsqueeze(2).broadcast_to([P, KC, NB]), ALU.mult), reads=[r_m2, r_const], writes=[r_m2])
            k.op(DVE, lambda: nc.vector.tensor_tensor(B2[:], B2[:], modT[:, 24:32, :], ALU.add), reads=[r_m2], writes=[r_m2])
            if debug:
                k.dma(SP, tl_dbg_for("modT"), [(dbg["modT"], modT[:])], reads=[r_m1, r_m2])

        def sc(t, m, b):
            return t[:, m, b:b + 1]

        def mm_group(bank, cols, pairs, reads, last_inc=True):
            n = len(pairs)
            for i, (lhsT, rhs) in enumerate(pairs):
                k.op(PE, lambda lhsT=lhsT, rhs=rhs, i=i: nc.tensor.matmul(
                    psum[bank][:, cols[0]:cols[1]], lhsT, rhs, start=(i == 0), stop=(i == n - 1)),
                    reads=reads, writes=[ps_reg[bank]], inc=(last_inc and i == n - 1))

        def tok_slice(g, gb):
            d, nb = DIL[g], NBLK[g]
            r, n = gb // nb, gb % nb
            start = r + d * 128 * n
            return slice(start, start + d * 127 + 1, d)

        T2f = T2[:].bitcast(F32)
        T2f_pitch = T2f.ap[0][0]
        s0_done = set()

        def emit_S0(b, early=False):
            if b in s0_done:
                return
            s0_done.add(b)
            for m in range(KC):
                if early:
                    q = m % 2
                    regs = aT2.R(q * 4096, 4096)
                    stg = AP(T2f.tensor, T2f.offset + q * 2048, [[T2f_pitch, P], [1, S]])
                else:
                    q = m % 4
                    regs = aT1.R(q * 2048, 2048)
                    stg = aT1.A(q * 2048, [[1, S]])
                k.dma(SP, tl_x[q], [(stg, xT[b, m * P:(m + 1) * P, :])], writes=regs)
                if m % 2 == 0:
                    k.op(ACT, lambda m=m, stg=stg: nc.scalar.activation(
                        hT[:, m, :], stg, AF.Identity, bias=sc(modT, m, b), scale=sc(A1, m, b)),
                        reads=regs + [r_m1], writes=r_hT[m])
                else:
                    k.op(DVE, lambda m=m, stg=stg: nc.vector.tensor_scalar(
                        hT[:, m, :], stg, sc(A1, m, b), sc(modT, m, b), ALU.mult, ALU.add),
                        reads=regs + [r_m1], writes=r_hT[m])

        def seq_body(b):
            emit_S0(b)
            if debug and b == 0:
                k.dma(SP, tl_dbg_for("hT"), [(dbg["hT"], hT[:])], reads=[r for rr in r_hT for r in rr])

            chk('S0')
            k.op(DVE, lambda: nc.vector.memset(aT3.A(64, [[384, 16], [192, 2], [1, 64]]), 1.0), writes=aT3.R(0, 6144))

            for g in range(3):
                d, nb = DIL[g], NBLK[g]
                item_begin()
                sq = wload(lambda s, g=g: [(wdst(s, KC, 0, 256), w_in_v[:, :, 256 * g:256 * g + 256])])
                sk = wload(lambda s, g=g: [(wdst(s, KC, 0, 256), w_in_v[:, :, 768 + 256 * g:768 + 256 * g + 256])])
                for mc in range(4):
                    slot = sq if mc < 2 else sk
                    c0 = (mc % 2) * 128
                    for tt in range(4):
                        bk = new_bank()
                        mm_group(bk, (0, 512), [(wv(slot, kk, c0, 128), hT[:, kk, tt * 512:(tt + 1) * 512]) for kk in range(KC)],
                                 reads=[r_w[slot]] + [r_hT[kk][tt] for kk in range(KC)])
                        if g == 0:
                            src = psum[bk][:, :]
                            dst = aT2.A(mc * 2048 + 512 * tt, [[1, 512]])
                        elif g == 1:
                            src = AP(psum[bk][:].tensor, psum[bk][:].offset, [[512, P], [1, 4], [4, 128]])
                            dst = aT2.A(mc * 2048 + 128 * tt, [[512, 4], [1, 128]])
                        else:
                            src = AP(psum[bk][:].tensor, psum[bk][:].offset, [[512, P], [1, 16], [16, 32]])
                            dst = aT2.A(mc * 2048 + 32 * tt, [[128, 16], [1, 32]])
                        regs = aT2.R(mc * 2048, 2048)
                        scl = 0.125 if mc < 2 else 1.0
                        e = k.evac_eng()
                        if e is ACT:
                            k.op(ACT, lambda dst=dst, src=src, scl=scl: nc.scalar.activation(dst, src, AF.Copy, scale=scl),
                                 reads=[ps_reg[bk]], writes=regs)
                        else:
                            k.op(DVE, lambda dst=dst, src=src, scl=scl: nc.vector.tensor_scalar(dst, src, scl, None, ALU.mult),
                                 reads=[ps_reg[bk]], writes=regs)
                chk('qk%d' % g)
                chk('v%d' % g)
                item_begin()
                sv = wload(lambda s, g=g: [(wdst(s, KC, 0, 256), w_in_v[:, :, 1536 + 256 * g:1536 + 256 * g + 256])])
                def vproj(gb, g=g, sv=sv):
                    bk = new_bank()
                    for j in range(2):
                        ts = tok_slice(g, gb + j)
                        tts = sorted(set([ts.start // 512, (ts.stop - 1) // 512])) if g == 0 else range(4)
                        mm_group(bk, (j * 256, j * 256 + 256), [(hT[:, kk, ts], wv(sv, kk, 0, 256)) for kk in range(KC)],
                                 reads=[r_w[sv]] + [r_hT[kk][t_] for kk in range(KC) for t_ in tts], last_inc=(j == 1))
                    for j in range(2):
                        src = AP(psum[bk][:].tensor, psum[bk][:].offset + j * 256, [[512, P], [128, 2], [64, 2], [1, 64]])
                        dst = aT3.A((gb + j) * 384, [[192, 2], [128, 2], [1, 64]])
                        e = k.evac_eng()
                        regs = aT3.R((gb + j) * 384, 384)
                        if e is ACT:
                            k.op(ACT, lambda dst=dst, src=src: nc.scalar.copy(dst, src), reads=[ps_reg[bk]], writes=regs)
                        else:
                            k.op(DVE, lambda dst=dst, src=src: nc.vector.tensor_copy(dst, src), reads=[ps_reg[bk]], writes=regs)

                def pt_ap(sidx, rel, dims, np_=P):
                    if sidx < 4:
                        return aT3.A(6144 + sidx * 512 + rel, dims)
                    return AP(T4b.tensor, T4b.offset + 6144 + (sidx - 4) * 512 + rel, [[T4b_pitch, np_]] + [list(d_) for d_ in dims])

                def pt_regs(sidx):
                    if sidx < 4:
                        return aT3.R(6144 + sidx * 512, 512)
                    return aT4.R(3072 + (sidx - 4) * 256, 256)

                for hp in range(2):
                    qreg = aT2.R(hp * 2048, 2048)
                    kreg = aT2.R((2 + hp) * 2048, 2048)

                    def att_scores(pr, hp=hp, qreg=qreg, kreg=kreg):
                        gb2 = 2 * pr
                        ws = [256 if ((gb2 + j) % nb) < nb - 1 else 128 for j in range(2)]
                        sbk = [new_bank(), new_bank()]
                        for j in range(2):
                            gb = gb2 + j
                            for e_ in range(2):
                                kT = aT2.A((2 + hp) * 2048 + gb * 128, [[1, 128]], p0=64 * e_, np_=64)
                                qT = aT2.A(hp * 2048 + gb * 128, [[1, ws[j]]], p0=64 * e_, np_=64)
                                k.op(PE, lambda kT=kT, qT=qT, e_=e_, j=j: nc.tensor.matmul(
                                    psum[sbk[e_]][:, j * 256:j * 256 + ws[j]], kT, qT, start=True, stop=True),
                                    reads=qreg + kreg, writes=[ps_reg[sbk[e_]]], inc=(j == 1))
                        for e_ in range(2):
                            h = g * 4 + 2 * hp + e_
                            sidx = 3 * e_ + pr % 3
                            treg = aT4.R(2048 + e_ * 512, 512)
                            preg = pt_regs(sidx)
                            if ws[0] == ws[1]:
                                segs = [(0, 2, ws[0])]
                            else:
                                segs = [(0, 1, ws[0]), (1, 1, ws[1])]
                            for (j0, nj, w) in segs:
                                tmp = aT4.A(2048 + e_ * 512 + j0 * 256, [[256, nj], [1, w]])
                                src = AP(psum[sbk[e_]][:].tensor, psum[sbk[e_]][:].offset + j0 * 256, [[512, P], [256, nj], [1, w]])
                                bsrc = bias_t[:, h:h + 1, 0:w].broadcast_to([P, nj, w])
                                k.op(DVE, lambda tmp=tmp, src=src, bsrc=bsrc: nc.vector.tensor_tensor(tmp, src, bsrc, ALU.add),
                                     reads=[ps_reg[sbk[e_]], r_const], writes=treg)
                                pt = pt_ap(sidx, j0 * 256, [[256, nj], [1, w]])
                                k.op(ACT, lambda pt=pt, tmp=tmp: nc.scalar.activation(pt, tmp, AF.Exp), reads=treg, writes=preg)

                    def att_out(pr, hp=hp):
                        gb2 = 2 * pr
                        ob = new_bank()
                        for e_ in range(2):
                            h = 2 * hp + e_
                            vc = (0, 64, 192, 256)[h]
                            scur = 3 * e_ + pr % 3
                            sprev = 3 * e_ + (pr - 1) % 3
                            for j in range(2):
                                gb = gb2 + j
                                n = gb % nb
                                col = e_ * 256 + j * 128
                                pairs = []
                                rds = []
                                if n > 0:
                                    if j == 0:
                                        pairs.append((aT3.A((gb - 1) * 384 + vc, [[1, 128]]), pt_ap(sprev, 256 + 128, [[1, 128]])))
                                        rds += pt_regs(sprev)
                                    else:
                                        pairs.append((aT3.A((gb - 1) * 384 + vc, [[1, 128]]), pt_ap(scur, 128, [[1, 128]])))
                                    rds += aT3.R((gb - 1) * 384, 384)
                                pairs.append((aT3.A(gb * 384 + vc, [[1, 128]]), pt_ap(scur, j * 256, [[1, 128]])))
                                rds += aT3.R(gb * 384, 384) + pt_regs(scur)
                                mm_group(ob, (col, col + 128), pairs, reads=rds, last_inc=(e_ == 1 and j == 1))
                        g0 = gb2
                        src = AP(psum[ob][:].tensor, psum[ob][:].offset, [[512, P], [256, 2], [1, 256]])
                        if g == 0:
                            dst = aT1.A(2 * hp * 2048 + 128 * g0, [[2048, 2], [1, 256]])
                        elif g == 1:
                            dst = aT1.A(2 * hp * 2048 + 512 * (g0 % 4) + g0 // 4, [[2048, 2], [4, 256]])
                        else:
                            src = AP(psum[ob][:].tensor, psum[ob][:].offset, [[512, P], [256, 2], [128, 2], [1, 128]])
                            dst = aT1.A(2 * hp * 2048 + g0, [[2048, 2], [1, 2], [16, 128]])
                        aregs = aT1.R(2 * hp * 2048, 4096)
                        if g == 0 and pr % 2 == 0:
                            k.op(ACT, lambda dst=dst, src=src: nc.scalar.copy(dst, src), reads=[ps_reg[ob]], writes=aregs)
                        elif g == 0:
                            k.op(DVE, lambda dst=dst, src=src: nc.vector.tensor_copy(dst, src), reads=[ps_reg[ob]], writes=aregs)
                        else:
                            k.op(DVE, lambda dst=dst, src=src: nc.vector.tensor_tensor(dst, src, dst, ALU.add),
                                 reads=[ps_reg[ob]] + aregs, writes=aregs)

                    for pr in range(9):
                        if pr < 8:
                            if hp == 0:
                                vproj(2 * pr)
                            if b == 0 and g < 2 and hp == 1:
                                emit_mod(8 + 8 * g + pr)
                            att_scores(pr)
                        if pr >= 1:
                            att_out(pr - 1)

                if b == 0 and g == 1:
                    emit_derived2()
                chk('att%d' % g)
            for hp in range(2):
                j0, j1 = 2 * hp, 2 * hp + 1
                a0 = aT1.R(j0 * 2048, 2048)
                a1 = aT1.R(j1 * 2048, 2048)
                rr = aT4.R(0, 2048)
                for (jj, p0_, ar) in ((j0, 64, a0), (j1, 0, a1)):
                    dv = aT1.A(jj * 2048, [[1, S]], p0_, 64)
                    k.op(ACT, lambda dv=dv: nc.scalar.activation(dv, dv, AF.Ln), reads=ar, writes=ar)
                    k.op(ACT, lambda dv=dv: nc.scalar.activation(dv, dv, AF.Exp, scale=-1.0), reads=ar, writes=ar)
                k.op(DVE, lambda: nc.vector.tensor_copy(aT4.A(0, [[1, S]], 0, 64), aT1.A(j0 * 2048, [[1, S]], 64, 64)), reads=a0, writes=rr)
                k.op(DVE, lambda: nc.vector.tensor_copy(aT4.A(0, [[1, S]], 64, 64), aT1.A(j1 * 2048, [[1, S]], 0, 64)), reads=a1, writes=rr)
                k.op(DVE, lambda: nc.vector.tensor_tensor(attT[0:64, hp, :], aT1.A(j0 * 2048, [[1, S]], 0, 64), aT4.A(0, [[1, S]], 0, 64), ALU.mult),
                     reads=a0 + rr, writes=r_attT[hp])
                k.op(DVE, lambda: nc.vector.tensor_tensor(attT[64:128, hp, :], aT1.A(j1 * 2048, [[1, S]], 64, 64), aT4.A(0, [[1, S]], 64, 64), ALU.mult),
                     reads=a1 + rr, writes=r_attT[hp])
            if debug and b == 0:
                k.dma(SP, tl_dbg_for("attT"), [(dbg["attT"], attT[:])], reads=[r for rr_ in r_attT for r in rr_])

            chk('att')
            for hf in range(2):
                t0 = hf * HALF
                seq_start = (hf == 0)

                def hreg(kk, tt):
                    return r_hT[kk][2 * hf + tt]

                def p1_proj(pg_):
                    item_begin()
                    sp_ = wload(lambda s, pg_=pg_: [(wdst(s, KC, 0, 256), w_in_v[:, :, 2304 + 256 * pg_:2304 + 256 * pg_ + 256])])
                    ui = pg_ % 2
                    taus = list(range(1 if seq_start else 0, 9))
                    i = 0
                    while i < len(taus):
                        bk = new_bank()
                        grp = taus[i:i + 2]
                        for j, tau in enumerate(grp):
                            tk0 = t0 + 128 * (tau - 1)
                            tt_abs = tk0 // 512
                            mm_group(bk, (j * 256, j * 256 + 256),
                                     [(hT[:, kk, tk0:tk0 + 128], wv(sp_, kk, 0, 256)) for kk in range(KC)],
                                     reads=[r_w[sp_]] + [r_hT[kk][tt_abs] for kk in range(KC)], last_inc=(j == len(grp) - 1))
                        e = k.evac_eng()
                        dst = u_t[:, ui, grp[0]:grp[0] + len(grp), :]
                        src = psum[bk][:, 0:256 * len(grp)].rearrange("p (a c) -> p a c", c=256)
                        if e is ACT:
                            k.op(ACT, lambda dst=dst, src=src: nc.scalar.copy(dst, src), reads=[ps_reg[bk]], writes=[r_u[ui]])
                        else:
                            k.op(DVE, lambda dst=dst, src=src: nc.vector.tensor_copy(dst, src), reads=[ps_reg[bk]], writes=[r_u[ui]])
                        i += 2

                def p1_bands(pg_):
                    wpool = POOL_W[pg_]
                    ui = pg_ % 2
                    for cc in range(2):
                        pc = 2 * pg_ + cc
                        for tq in range(2):
                            bk = new_bank()
                            for s_ in range(4):
                                tau = 1 + 4 * tq + s_
                                first = seq_start and tau == 1
                                pairs = [(u_t[:, ui, tau, cc * 128:(cc + 1) * 128], bands_t[:, pg_ * 3 + (2 if first else 0), :])]
                                if not first:
                                    pairs.append((u_t[:, ui, tau - 1, cc * 128:(cc + 1) * 128], bands_t[:, pg_ * 3 + 1, :]))
                                mm_group(bk, (s_ * 128, s_ * 128 + 128), pairs, reads=[r_u[ui], r_bands], last_inc=(s_ == 3))
                            dst = aT2.A(pc * 1024 + tq * 512, [[1, 512]])
                            regs = aT2.R(pc * 1024 + tq * 512, 512)
                            e = k.evac_eng()
                            if e is ACT:
                                k.op(ACT, lambda dst=dst, bk=bk: nc.scalar.activation(dst, psum[bk][:, :], AF.Copy, scale=1.0 / wpool),
                                     reads=[ps_reg[bk]], writes=regs)
                            else:
                                k.op(DVE, lambda dst=dst, bk=bk: nc.vector.tensor_scalar(dst, psum[bk][:, :], 1.0 / wpool, None, ALU.mult),
                                     reads=[ps_reg[bk]], writes=regs)
                            if seq_start and tq == 0:
                                k.op(DVE, lambda bk=bk, pc=pc: nc.vector.tensor_tensor(
                                    aT2.A(pc * 1024, [[1, 16]]), psum[bk][:, 0:16], corr_t[:, pg_, :], ALU.mult),
                                    reads=[ps_reg[bk], r_const], writes=regs)
                for pg_ in range(4):
                    p1_proj(pg_)
                    if pg_ >= 1:
                        p1_bands(pg_ - 1)
                p1_bands(3)
                if debug and b == 0 and hf == 0:
                    k.dma(SP, tl_dbg_for("pmT"), [(dbg["pmT"], aT2.A(0, [[1024, KC], [1, HALF]]))], reads=aT2.R(0, 8192))

                chk('P1')
                item_begin()
                spg = wload(lambda s: [(wring[:, s, ci * 1024:(ci + 1) * 1024].rearrange("p (g o) -> p g o", g=4), w_pg_v[:, ci, :, :]) for ci in range(2)])
                for co in range(KC):
                    pg_, cc = co // 2, co % 2
                    for tt in range(2):
                        bk = new_bank()
                        pairs = [(wring[:, spg, ci * 1024 + pg_ * 256 + cc * 128: ci * 1024 + pg_ * 256 + cc * 128 + 128],
                                  aT2.A((2 * pg_ + ci) * 1024 + tt * 512, [[1, 512]])) for ci in range(2)]
                        rds = [r_w[spg]] + aT2.R((2 * pg_) * 1024 + tt * 512, 512) + aT2.R((2 * pg_ + 1) * 1024 + tt * 512, 512)
                        mm_group(bk, (0, 512), pairs, reads=rds)
                        dst = aT3.A(co * 1024 + tt * 512, [[1, 512]])
                        regs = aT3.R(co * 1024 + tt * 512, 512)
                        e = k.evac_eng()
                        if e is ACT:
                            k.op(ACT, lambda dst=dst, bk=bk, co=co: nc.scalar.activation(dst, psum[bk][:, :], AF.Copy, scale=prm[:, 4, co:co + 1]),
                                 reads=[ps_reg[bk], r_const], writes=regs)
                        else:
                            k.op(DVE, lambda dst=dst, bk=bk, co=co: nc.vector.tensor_scalar(dst, psum[bk][:, :], prm[:, 4, co:co + 1], None, ALU.mult),
                                 reads=[ps_reg[bk], r_const], writes=regs)
                if debug and b == 0 and hf == 0:
                    k.dma(SP, tl_dbg_for("pgT"), [(dbg["pgT"], aT3.A(0, [[1024, KC], [1, HALF]]))], reads=aT3.R(0, 8192))

                chk('P2')
                for m in range(KC):
                    yreg_all = aT1.R(m * 1024, 1024)
                    k.dma(SP, tl_y[m], [(aT1.A(m * 1024, [[1, HALF]]), xT[b, m * P:(m + 1) * P, t0:t0 + HALF])], writes=yreg_all)
                    k.op(ACT, lambda m=m: nc.scalar.activation(
                        aT1.A(m * 1024, [[1, HALF]]), aT1.A(m * 1024, [[1, HALF]]), AF.Copy, scale=ALPHA),
                        reads=yreg_all, writes=yreg_all)
                for m2 in range(4):
                    item_begin()
                    sga = wload(lambda s, m2=m2: [(wdst(s, KC, 0, 256), w_in_v[:, :, 3328 + 256 * m2:3328 + 256 * m2 + 256])])
                    sgb = wload(lambda s, m2=m2: [(wdst(s, KC, 0, 256), w_in_v[:, :, 4352 + 256 * m2:4352 + 256 * m2 + 256])])
                    sbp = wload(lambda s, m2=m2: [(wdst(s, KC, 0, 256), w_bp_v[:, :, 256 * m2:256 * m2 + 256])])
                    for mi in range(2):
                        m = 2 * m2 + mi
                        for tt in range(2):
                            tk = slice(t0 + tt * 512, t0 + tt * 512 + 512)
                            hrd = [hreg(kk, tt) for kk in range(KC)]
                            b_ga = new_bank()
                            mm_group(b_ga, (0, 512), [(wv(sga, kk, mi * 128, 128), hT[:, kk, tk]) for kk in range(KC)], reads=[r_w[sga]] + hrd)
                            b_gb = new_bank()
                            mm_group(b_gb, (0, 512), [(wv(sgb, kk, mi * 128, 128), hT[:, kk, tk]) for kk in range(KC)], reads=[r_w[sgb]] + hrd)
                            b_a = new_bank()
                            mm_group(b_a, (0, 512), [(watt[:, kc, m * 128:(m + 1) * 128], attT[:, kc, tk]) for kc in range(2)],
                                     reads=[r_watt] + [r_attT[kc][2 * hf + tt] for kc in range(2)])
                            b_b = new_bank()
                            mm_group(b_b, (0, 512), [(wv(sbp, kk, mi * 128, 128), aT3.A(kk * 1024 + tt * 512, [[1, 512]])) for kk in range(KC)],
                                     reads=[r_w[sbp]] + [r_ for kk in range(KC) for r_ in aT3.R(kk * 1024 + tt * 512, 512)])
                            si = 2 * (tt % 2)
                            k.op(ACT, lambda b_ga=b_ga, si=si: nc.scalar.activation(sg_t[:, si, :], psum[b_ga][:, :], AF.Sigmoid),
                                 reads=[ps_reg[b_ga]], writes=[r_sg[si]])
                            k.op(ACT, lambda b_gb=b_gb, si=si: nc.scalar.activation(sg_t[:, si + 1, :], psum[b_gb][:, :], AF.Sigmoid),
                                 reads=[ps_reg[b_gb]], writes=[r_sg[si + 1]])
                            k.op(DVE, lambda b_a=b_a, si=si: nc.vector.tensor_tensor(t12_t[:, 0, :], psum[b_a][:, :], sg_t[:, si, :], ALU.mult),
                                 reads=[ps_reg[b_a], r_sg[si]], writes=[r_t12[0]])
                            k.op(DVE, lambda b_b=b_b, si=si: nc.vector.tensor_tensor(t12_t[:, 1, :], psum[b_b][:, :], sg_t[:, si + 1, :], ALU.mult),
                                 reads=[ps_reg[b_b], r_sg[si + 1]], writes=[r_t12[1]])
                            mreg = aT2.R(m * 1024 + tt * 512, 512)
                            k.op(DVE, lambda m=m, tt=tt: nc.vector.tensor_tensor(
                                aT2.A(m * 1024 + tt * 512, [[1, 512]]), t12_t[:, 0, :], t12_t[:, 1, :], ALU.add),
                                reads=r_t12, writes=mreg)
                if debug and b == 0 and hf == 0:
                    k.dma(SP, tl_dbg_for("mrg"), [(dbg["mrg"], aT2.A(0, [[1024, KC], [1, HALF]]))], reads=aT2.R(0, 8192))

                chk('D1')
                stat = [[new_bank(hold=True) for _ in range(2)] for _ in range(2)]

                feed = {"n": 0, "q": []}

                def ln_stats_feed(m, tt, last):
                    yreg = aT1.R(m * 1024 + tt * 512, 512)
                    ysrc = aT1.A(m * 1024 + tt * 512, [[1, 512]])
                    i0 = 2 * (feed["n"] % 3)
                    feed["n"] += 1
                    k.op(ACT, lambda: nc.scalar.copy(ybs_t[:, i0, :], ysrc), reads=yreg, writes=[r_ybs[i0]])
                    k.op(ACT, lambda: nc.scalar.activation(ybs_t[:, i0 + 1, :], ysrc, AF.Square), reads=yreg, writes=[r_ybs[i0 + 1]])
                    feed["q"].append((m, tt, last, i0))
                    while len(feed["q"]) > 2:
                        ln_stats_pe(*feed["q"].pop(0))

                def ln_stats_flush():
                    while feed["q"]:
                        ln_stats_pe(*feed["q"].pop(0))

                def ln_stats_pe(m, tt, last, i0):
                    for j in range(2):
                        bk = stat[tt][j]
                        k.op(PE, lambda bk=bk, j=j: nc.tensor.matmul(psum[bk][:, :], onesm[:], ybs_t[:, i0 + j, :], start=(m == 0), stop=last),
                             reads=[r_ybs[i0 + j], r_const], writes=[ps_reg[bk]], inc=True)

                def ln_finish(tt):
                    bm, bq = stat[tt]
                    k.op(ACT, lambda: nc.scalar.activation(st_t[:, 0, :], psum[bm][:, :], AF.Square), reads=[ps_reg[bm]], writes=[r_st[0]])
                    k.op(DVE, lambda: nc.vector.tensor_tensor(st_t[:, 1, :], psum[bq][:, :], st_t[:, 0, :], ALU.subtract),
                         reads=[ps_reg[bq], r_st[0]], writes=[r_st[1]])
                    k.op(ACT, lambda: nc.scalar.activation(st_t[:, 1, :], st_t[:, 1, :], AF.Ln, bias=eps_t[:, 0:1]),
                         reads=[r_st[1], r_const], writes=[r_st[1]])
                    k.op(ACT, lambda: nc.scalar.activation(st_t[:, 2, :], st_t[:, 1, :], AF.Exp, scale=-0.5), reads=[r_st[1]], writes=[r_st[2]])
                    k.op(DVE, lambda: nc.vector.scalar_tensor_tensor(st_t[:, 3, :], psum[bm][:, :], -1.0, st_t[:, 2, :], ALU.mult, ALU.mult),
                         reads=[ps_reg[bm], r_st[2]], writes=[r_st[3]])
                    ps_held[bm] = False
                    ps_held[bq] = False

                def ln_norm(m, tt, ti):
                    yreg = aT1.R(m * 1024 + tt * 512, 512)
                    ysrc = aT1.A(m * 1024 + tt * 512, [[1, 512]])
                    k.op(DVE, lambda: nc.vector.tensor_tensor(t12_t[:, ti, :], ysrc, st_t[:, 2, :], ALU.mult),
                         reads=yreg + [r_st[2]], writes=[r_t12[ti]])
                    k.op(DVE, lambda: nc.vector.tensor_tensor(t12_t[:, ti, :], t12_t[:, ti, :], st_t[:, 3, :], ALU.add),
                         reads=[r_t12[ti], r_st[3]], writes=[r_t12[ti]])

                def ln1_apply_chunk(tt, m):
                    ti = m % 2
                    ln_norm(m, tt, ti)
                    yreg = aT1.R(m * 1024 + tt * 512, 512)
                    ydst = aT1.A(m * 1024 + tt * 512, [[1, 512]])
                    hreg2 = aT3.R(m * 1024 + tt * 512, 512)
                    k.op(ACT, lambda: nc.scalar.activation(
                        ydst, t12_t[:, ti, :], AF.Identity, bias=aGB[:, 1, m:m + 1], scale=aGB[:, 0, m:m + 1]),
                        reads=[r_t12[ti], r_m1], writes=yreg)
                    k.op(ACT, lambda: nc.scalar.activation(
                        aT3.A(m * 1024 + tt * 512, [[1, 512]]), t12_t[:, ti, :], AF.Identity, bias=sc(B2, m, b), scale=sc(G2, m, b)),
                        reads=[r_t12[ti], r_m2], writes=hreg2)

                item_begin()
                sos = [wload(lambda s, m2=m2: [(wdst(s, KC, 0, 256), w_out_v[:, :, 256 * m2:256 * m2 + 256])]) for m2 in range(4)]
                for tt in range(2):
                    for m in range(KC):
                        so, mi = sos[m // 2], m % 2
                        bk = new_bank()
                        mm_group(bk, (0, 512), [(wv(so, kk, mi * 128, 128), aT2.A(kk * 1024 + tt * 512, [[1, 512]])) for kk in range(KC)],
                                 reads=[r_w[so]] + [r_ for kk in range(KC) for r_ in aT2.R(kk * 1024 + tt * 512, 512)])
                        yreg = aT1.R(m * 1024 + tt * 512, 512)
                        ydst = aT1.A(m * 1024 + tt * 512, [[1, 512]])
                        k.op(DVE, lambda bk=bk, ydst=ydst, m=m: nc.vector.scalar_tensor_tensor(
                            ydst, psum[bk][:, :], sc(modT, 16 + m, b), ydst, ALU.mult, ALU.add),
                            reads=[ps_reg[bk], r_m2] + yreg, writes=yreg)
                        ln_stats_feed(m, tt, last=(m == KC - 1))
                        if tt == 1:
                            if m == 1:
                                ln_finish(0)
                            if m >= 2:
                                ln1_apply_chunk(0, m - 2)
                ln1_apply_chunk(0, KC - 2)
                ln1_apply_chunk(0, KC - 1)
                ln_stats_flush()
                ln_finish(1)
                for m in range(KC):
                    ln1_apply_chunk(1, m)
                if debug and b == 0 and hf == 0:
                    k.dma(SP, tl_dbg_for("y1h2"), [(dbg["y1"], aT1.A(0, [[1024, KC], [1, HALF]])), (dbg["h2"], aT3.A(0, [[1024, KC], [1, HALF]]))],
                          reads=aT1.R(0, 8192) + aT3.R(0, 8192))

                if hf == 1 and b + 1 < NB:
                    emit_S0(b + 1, early=True)
                chk('D2')
                for fgi, (f0, f1) in enumerate(FGROUPS):
                    nf = f1 - f0
                    fl = 0
                    while fl < nf:
                        nch = min(4 if (fgi == 0 and fl == 0) else 2, nf - fl)
                        item_begin()
                        sl_g, sl_u = [], []
                        for c_ in range(0, nch, 2):
                            ncols = min(2, nch - c_) * 128
                            c0 = (f0 + fl + c_) * 128
                            sl_g.append(wload(lambda s, c0=c0, ncols=ncols: [(wdst(s, KC, 0, ncols), w_gate_v[:, :, c0:c0 + ncols])]))
                            sl_u.append(wload(lambda s, c0=c0, ncols=ncols: [(wdst(s, KC, 0, ncols), w_up_v[:, :, c0:c0 + ncols])]))
                        for tt in range(2):
                            for fi in range(nch):
                                sgt, sup, fo = sl_g[fi // 2], sl_u[fi // 2], (fi % 2) * 128
                                h2r = [r_ for kk in range(KC) for r_ in aT3.R(kk * 1024 + tt * 512, 512)]
                                bg = new_bank()
                                mm_group(bg, (0, 512), [(wv(sgt, kk, fo, 128), aT3.A(kk * 1024 + tt * 512, [[1, 512]])) for kk in range(KC)],
                                         reads=[r_w[sgt]] + h2r)
                                bu = new_bank()
                                mm_group(bu, (0, 512), [(wv(sup, kk, fo, 128), aT3.A(kk * 1024 + tt * 512, [[1, 512]])) for kk in range(KC)],
                                         reads=[r_w[sup]] + h2r)
                                si = (2 * fi + tt) % 4
                                k.op(ACT, lambda bg=bg, si=si: nc.scalar.activation(sg_t[:, si, :], psum[bg][:, :], AF.Silu),
                                     reads=[ps_reg[bg]], writes=[r_sg[si]])
                                areg = aT4.R(((fl + fi) * 1024 + tt * 512) // 2, 256)
                                k.op(DVE, lambda bu=bu, si=si, fl=fl, fi=fi, tt=tt: nc.vector.tensor_tensor(
                                    actT_ap(fl + fi, tt * 512, 512), psum[bu][:, :], sg_t[:, si, :], ALU.mult),
                                    reads=[ps_reg[bu], r_sg[si]], writes=areg)
                        fl += nch
                    last_group = (fgi == len(FGROUPS) - 1)

                    def dn_step(sd, mi, m, tt, nf=nf):
                        bk = new_bank()
                        mm_group(bk, (0, 512), [(wv(sd, kf, mi * 128, 128), actT_ap(kf, tt * 512, 512)) for kf in range(nf)],
                                 reads=[r_w[sd]] + [r_ for kf in range(nf) for r_ in aT4.R((kf * 1024 + tt * 512) // 2, 256)])
                        yreg = aT1.R(m * 1024 + tt * 512, 512)
                        ydst = aT1.A(m * 1024 + tt * 512, [[1, 512]])
                        k.op(DVE, lambda: nc.vector.scalar_tensor_tensor(
                            ydst, psum[bk][:, :], sc(modT, 40 + m, b), ydst, ALU.mult, ALU.add),
                            reads=[ps_reg[bk], r_m2] + yreg, writes=yreg)

                    if not last_group:
                        for m2 in range(4):
                            item_begin()
                            sd = wload(lambda s, nf=nf, f0=f0, f1=f1, m2=m2: [(wdst(s, nf, 0, 256), w_down_v[:, f0:f1, 256 * m2:256 * m2 + 256])])
                            for mi in range(2):
                                for tt in range(2):
                                    dn_step(sd, mi, 2 * m2 + mi, tt)
                    else:
                        stat = [[new_bank(hold=True) for _ in range(2)] for _ in range(2)]
                        oi_ = [0]

                        def ln2_apply_chunk(tt, m):
                            ti = m % 2
                            ln_norm(m, tt, ti)
                            oi_[0] = (oi_[0] + 1) % 2
                            oi = oi_[0]
                            k.op(ACT, lambda: nc.scalar.activation(
                                ost_t[:, oi, :], t12_t[:, ti, :], AF.Identity, bias=prm[:, 3, m:m + 1], scale=prm[:, 2, m:m + 1]),
                                reads=[r_t12[ti], r_const], writes=[r_ost[oi]])
                            k.dma(SP, tl_o[oi], [(outT[b, m * P:(m + 1) * P, t0 + tt * 512:t0 + tt * 512 + 512], ost_t[:, oi, :])],
                                  reads=[r_ost[oi]])

                        item_begin()
                        sds = [wload(lambda s, nf=nf, f0=f0, f1=f1, m2=m2: [(wdst(s, nf, 0, 256), w_down_v[:, f0:f1, 256 * m2:256 * m2 + 256])])
                               for m2 in range(4)]
                        for tt in range(2):
                            for m in range(KC):
                                dn_step(sds[m // 2], m % 2, m, tt)
                                ln_stats_feed(m, tt, last=(m == KC - 1))
                                if tt == 1:
                                    if m == 1:
                                        ln_finish(0)
                                    if m >= 2:
                                        ln2_apply_chunk(0, m - 2)
                        ln2_apply_chunk(0, KC - 2)
                        ln2_apply_chunk(0, KC - 1)
                        chk('FFN')
                        ln_stats_flush()
                        ln_finish(1)
                        for m in range(KC):
                            ln2_apply_chunk(1, m)

        def emit_all():
            try:
                emit_consts()
                for it_ in range(8):
                    emit_mod(it_)
                emit_derived1()
                chk('mod')
                for b_ in range(NB):
                    seq_body(b_)
            except _Stop:
                pass

        k.dry = True
        emit_all()
        k.dry = False
        ps_next[0] = 0
        for i_ in range(8):
            ps_held[i_] = False
        k.flip = 0
        s0_done.clear()
        emit_all()

        for tl in tl_o + list(tl_dbgs.values()):
            if tl.cnt:
                nc.sync.wait_ge(tl.sem, tl.cnt)
    return nc


_NC_CACHE = {}


def _get_nc(debug=False):
    if debug not in _NC_CACHE:
        _NC_CACHE[debug] = build_program(debug)
    return _NC_CACHE[debug]


def make_in_maps(inputs):
    f = lambda a: np.ascontiguousarray(np.asarray(a, dtype=np.float32))
    x = f(inputs["x"])
    c = f(inputs["c"])
    vecT = lambda v, n: f(np.asarray(v, np.float32).reshape(n, P).T)
    bias, bands, corr = const_tables()
    shared = {
        "w_ada": f(inputs["w_ada"][0]), "b_adaT": vecT(inputs["b_ada"][0], 48),
        "w_in": f(inputs["w_in"][0]), "w_batt": f(inputs["w_branch_att"][0]),
        "w_pg": f(inputs["w_pool_group"][0]), "pscaleT": vecT(inputs["pool_scale"][0], KC),
        "w_bp": f(inputs["w_branch_pool"][0]), "w_out": f(inputs["w_out"][0]),
        "ln1gT": vecT(inputs["ln1_g"][0], KC), "ln1bT": vecT(inputs["ln1_b"][0], KC),
        "ln2gT": vecT(inputs["ln2_g"][0], KC), "ln2bT": vecT(inputs["ln2_b"][0], KC),
        "w_gate": f(inputs["w_gate"][0]), "w_up": f(inputs["w_up"][0]), "w_down": f(inputs["w_down"][0]),
        "bias_tab": bias, "bands": bands, "corr": corr,
    }
    in_maps = []
    for i in range(N_CORES):
        xs = x[NB * i:NB * (i + 1)]
        m = dict(shared)
        m["xT"] = f(xs.transpose(0, 2, 1))
        cs = c[NB * i:NB * (i + 1)]
        m["cT"] = f(cs.reshape(NB, KC, P).transpose(2, 1, 0))
        in_maps.append(m)
    return in_maps


def kernel(**inputs):
    nc = _get_nc(False)
    in_maps = make_in_maps(inputs)
    res = run_bass_kernel_spmd(nc, in_maps, core_ids=list(range(N_CORES)))
    out = np.empty((NB * N_CORES, S, D), np.float32)
    for i in range(N_CORES):
        out[NB * i:NB * (i + 1)] = np.asarray(res.results[i]["outT"]).transpose(0, 2, 1)
    return out
```
